# Optimizing a Trainium2 kernel written in Bass

```python
import math
import jax, jax.numpy as jnp
from jax import lax
import numpy as np

D_MODEL = 1024
BATCH = 4
SEQ = 4096
DEPTH = 1

A_HEADS = 8
A_V_DIM = 64
A_WIDTH = A_HEADS * A_V_DIM
QK_NOPE_DIM = 64
QK_ROPE_DIM = 32
QK_DIM = QK_NOPE_DIM + QK_ROPE_DIM
Q_LORA_RANK = 256
KV_LORA_RANK = 128
ROPE_THETA = 10000.0
Q_BLOCK = 128
B_HEADS = 8
B_HEAD_DIM = 64
B_WIDTH = B_HEADS * B_HEAD_DIM
CHUNK = 128
D_MIX = A_WIDTH + B_WIDTH

IN_SPLITS = (Q_LORA_RANK, KV_LORA_RANK, QK_ROPE_DIM, A_WIDTH, B_WIDTH, B_WIDTH, B_WIDTH)
D_IN = Q_LORA_RANK + KV_LORA_RANK + QK_ROPE_DIM + A_WIDTH + 3 * B_WIDTH
EPS = 1e-6

kernel_name = "hybrid_mla_gmlp_parallel_groups"


def rms_norm(x, g):
    xf = x.astype(jnp.float32)
    y = xf * lax.rsqrt(jnp.mean(xf * xf, axis=-1, keepdims=True) + EPS)
    return (y * g.astype(jnp.float32)).astype(x.dtype)


def rope_cos_sin(positions):
    inv_freq = 1.0 / (ROPE_THETA ** (jnp.arange(0, QK_ROPE_DIM, 2, dtype=jnp.float32) / QK_ROPE_DIM))
    ang = positions.astype(jnp.float32)[..., None] * inv_freq
    return jnp.cos(ang)[:, :, None, :], jnp.sin(ang)[:, :, None, :]


def apply_rope(t, cos, sin):
    tf = t.astype(jnp.float32)
    t1, t2 = jnp.split(tf, 2, axis=-1)
    out = jnp.concatenate([t1 * cos - t2 * sin, t2 * cos + t1 * sin], axis=-1)
    return out.astype(t.dtype)


def split_cols(t, sizes):
    offs = np.cumsum(sizes)[:-1].tolist()
    return jnp.split(t, offs, axis=-1)


def setup_inputs(seed: int = 0) -> dict:
    key = jax.random.key(seed)
    ks = jax.random.split(key, 18)
    f32 = jnp.float32
    x = jax.random.normal(ks[0], (BATCH, SEQ, D_MODEL), f32)
    offset = jax.random.randint(ks[1], (BATCH, 1), 0, 1024, dtype=jnp.int32)
    positions = (jnp.arange(SEQ, dtype=jnp.int32)[None, :] + offset).astype(jnp.int32)
    gain = lambda k, shape: 1.0 + 0.02 * jax.random.normal(k, shape, f32)
    return {
        "x": x,
        "positions": positions,
        "norm_in_g": gain(ks[2], (D_MODEL,)),
        "w_in": jax.random.normal(ks[3], (D_MODEL, D_IN), f32) * D_MODEL ** -0.5,
        "q_lora_g": gain(ks[4], (Q_LORA_RANK,)),
        "w_uq": jax.random.normal(ks[5], (Q_LORA_RANK, A_HEADS * QK_DIM), f32) * Q_LORA_RANK ** -0.5,
        "kv_lora_g": gain(ks[6], (KV_LORA_RANK,)),
        "w_ukv": jax.random.normal(ks[7], (KV_LORA_RANK, A_HEADS * (QK_NOPE_DIM + A_V_DIM)), f32) * KV_LORA_RANK ** -0.5,
        "q_head_g": gain(ks[8], (QK_DIM,)),
        "k_head_g": gain(ks[9], (QK_DIM,)),
        "v_gate_g": gain(ks[10], (B_HEADS, B_HEAD_DIM)),
        "w_s": jax.random.normal(ks[11], (B_HEADS, CHUNK, CHUNK), f32) * CHUNK ** -0.5,
        "b_s": 0.02 * jax.random.normal(ks[12], (B_HEADS, CHUNK), f32),
        "out_a_g": gain(ks[13], (A_WIDTH,)),
        "out_b_g": gain(ks[14], (B_WIDTH,)),
        "w_out": jax.random.normal(ks[15], (D_MIX, D_MODEL), f32) * D_MIX ** -0.5,
    }


def mla_group(c_q, c_kv, k_rope, cos, sin, q_lora_g, w_uq, kv_lora_g, w_ukv, q_head_g, k_head_g):
    B, S, _ = c_q.shape
    q = (rms_norm(c_q, q_lora_g) @ w_uq).reshape(B, S, A_HEADS, QK_DIM)
    q_nope, q_pe = q[..., :QK_NOPE_DIM], q[..., QK_NOPE_DIM:]
    q = jnp.concatenate([q_nope, apply_rope(q_pe, cos, sin)], axis=-1)
    kv = (rms_norm(c_kv, kv_lora_g) @ w_ukv).reshape(B, S, A_HEADS, QK_NOPE_DIM + A_V_DIM)
    k_nope, v = kv[..., :QK_NOPE_DIM], kv[..., QK_NOPE_DIM:]
    k_pe = apply_rope(k_rope[:, :, None, :], cos, sin)
    k = jnp.concatenate([k_nope, jnp.broadcast_to(k_pe, (B, S, A_HEADS, QK_ROPE_DIM))], axis=-1)
    q = rms_norm(q, q_head_g)
    k = rms_norm(k, k_head_g)
    q = jnp.transpose(q, (0, 2, 1, 3))
    k = jnp.transpose(k, (0, 2, 1, 3))
    v = jnp.transpose(v, (0, 2, 1, 3))
    scale = 1.0 / math.sqrt(QK_DIM)
    n_blk = S // Q_BLOCK
    q_blocks = jnp.moveaxis(q.reshape(B, A_HEADS, n_blk, Q_BLOCK, QK_DIM), 2, 0)

    def attend(qb):
        s = jnp.einsum("bhqd,bhkd->bhqk", qb, k).astype(jnp.float32) * scale
        p = jax.nn.softmax(s, axis=-1).astype(v.dtype)
        return jnp.einsum("bhqk,bhkd->bhqd", p, v)

    o = lax.map(attend, q_blocks)
    o = jnp.transpose(o, (1, 0, 3, 2, 4)).reshape(B, S, A_WIDTH)
    return o


def gmlp_group(u, v, v_gate_g, w_s, b_s):
    B, S, _ = u.shape
    n_chunk = S // CHUNK
    u = jax.nn.gelu(u)
    v = rms_norm(jax.nn.gelu(v).reshape(B, S, B_HEADS, B_HEAD_DIM), v_gate_g)
    v = v.reshape(B, n_chunk, CHUNK, B_HEADS, B_HEAD_DIM)
    sv = jnp.einsum("hij,bcjhd->bcihd", w_s, v) + jnp.transpose(b_s)[None, None, :, :, None]
    out = u.reshape(B, n_chunk, CHUNK, B_HEADS, B_HEAD_DIM) * sv
    return out.reshape(B, S, B_WIDTH)


def reference(x, positions, norm_in_g, w_in, q_lora_g, w_uq, kv_lora_g, w_ukv, q_head_g, k_head_g,
              v_gate_g, w_s, b_s, out_a_g, out_b_g, w_out):
    cos, sin = rope_cos_sin(positions)
    for _ in range(DEPTH):
        h = rms_norm(x, norm_in_g)
        proj = h @ w_in
        c_q, c_kv, k_rope, z_a, u, v, z_b = split_cols(proj, IN_SPLITS)
        o_a = mla_group(c_q, c_kv, k_rope, cos, sin, q_lora_g, w_uq, kv_lora_g, w_ukv, q_head_g, k_head_g)
        o_b = gmlp_group(u, v, v_gate_g, w_s, b_s)
        o_a = rms_norm(o_a, out_a_g) * jax.nn.silu(z_a)
        o_b = rms_norm(o_b, out_b_g) * jax.nn.silu(z_b)
        x = x + jnp.concatenate([o_a, o_b], axis=-1) @ w_out
    return x
```

```python
import math
from contextlib import ExitStack

import numpy as np
import concourse.bass as bass
import concourse.mybir as mybir
from concourse.bass_utils import run_bass_kernel_spmd

F32 = mybir.dt.float32
BF16 = mybir.dt.bfloat16
I32 = mybir.dt.int32
AF = mybir.ActivationFunctionType
ALU = mybir.AluOpType
AX = mybir.AxisListType

D_MODEL = 1024
SEQ = 4096
NLOC = 2048
NT = 32
NTL = 16
D_IN = 2464
EPS = 1e-6
TWO_PI = 2.0 * math.pi
CW1 = 6.28125
CW2 = TWO_PI - CW1
PI_LO = 3.1415925


class Sem:
    def __init__(self, h, name):
        self.h = h
        self.n = 0
        self.name = name


class Buf:
    __slots__ = ("w", "r", "name", "excl")

    def __init__(self, name="", excl=False):
        self.w = None
        self.r = {}
        self.name = name
        self.excl = excl


class Eng:
    def __init__(self, e, sem, name):
        self.e = e
        self.sem = sem
        self.clock = {}
        self.name = name


class Ctx:
    def __init__(self, nc, es):
        self.nc = nc
        self.es = es
        self.sems = []
        self.snap = {}
        self.pe = Eng(nc.tensor, self.newsem("s_pe"), "pe")
        self.act = Eng(nc.scalar, self.newsem("s_act"), "act")
        self.dve = Eng(nc.vector, self.newsem("s_dve"), "dve")
        self.pool = Eng(nc.gpsimd, self.newsem("s_pool"), "pool")
        self.sp = Eng(nc.sync, None, "sp")
        self.engs = [self.pe, self.act, self.dve, self.pool, self.sp]

    def newsem(self, name):
        s = Sem(self.es.enter_context(self.nc.semaphore(name)), name)
        self.sems.append(s)
        return s

    def _waits(self, eng, rd, wr, skip_same, waw_implied=False):
        need = {}

        def add(tok, same_ok):
            if tok is None:
                return
            s, v = tok
            if skip_same and s is eng.sem and not same_ok:
                return
            if need.get(s, 0) < v:
                need[s] = v

        strict = eng is not self.pe
        for b in rd:
            add(b.w, strict)
            if b.excl:
                for s, v in b.r.items():
                    add((s, v), False)
        for b in wr:
            add(b.w, strict and not waw_implied)
            for s, v in b.r.items():
                add((s, v), strict)
        for s, v in sorted(need.items(), key=lambda kv: kv[0] is eng.sem):
            assert v <= s.n, f"wait on unsignalled token {s.name} {v}>{s.n} ({eng.name})"
            if eng.clock.get(s, 0) >= v:
                continue
            eng.e.wait_ge(s.h, v)
            self._learn(eng, s, v)

    def _learn(self, eng, s, v):
        if eng.clock.get(s, 0) < v:
            eng.clock[s] = v
        for s2, v2 in self.snap.get((s, v), {}).items():
            if eng.clock.get(s2, 0) < v2:
                eng.clock[s2] = v2

    def _record(self, tok, rd, wr):
        s, v = tok
        for b in rd:
            if b.r.get(s, 0) < v:
                b.r[s] = v
        for b in wr:
            b.w = tok
            b.r = {}

    def op(self, eng, fn, rd=(), wr=(), sig=True, waw_implied=False):
        self._waits(eng, rd, wr, True, waw_implied)
        ins = fn()
        if sig:
            eng.sem.n += 1
            ins.then_inc(eng.sem.h, 1)
            tok = (eng.sem, eng.sem.n)
            self.snap[tok] = dict(eng.clock)
        else:
            tok = (eng.sem, eng.sem.n + 1)
        self._record(tok, rd, wr)
        return ins

    def dma(self, eng, out, in_, sem, rd=(), wr=()):
        self._waits(eng, rd, wr, False)
        ins = eng.e.dma_start(out=out, in_=in_)
        sem.n += 16
        ins.then_inc(sem.h, 16)
        self.snap[(sem, sem.n)] = dict(eng.clock)
        self._record((sem, sem.n), rd, wr)
        return ins

    def barrier(self):
        for eng in self.engs:
            for s in self.sems:
                if s.n > 0 and eng.clock.get(s, 0) < s.n:
                    eng.e.wait_ge(s.h, s.n)
                    self._learn(eng, s, s.n)


def build_program():
    nc = bass.Bass("TRN2", target_bir_lowering=False)
    dt = lambda name, shape, dty, kind: nc.dram_tensor(name, shape, dty, kind=kind).ap()
    x_d = dt("x", [SEQ, D_MODEL], F32, "ExternalInput")
    pos_d = dt("pos", [128, NT], I32, "ExternalInput")
    pack1_d = dt("pack1", [128, 171], F32, "ExternalInput")
    pack2_d = dt("pack2", [128, 704], F32, "ExternalInput")
    win_d = dt("w_in", [D_MODEL, D_IN], F32, "ExternalInput")
    wuq_d = dt("w_uq", [256, 768], F32, "ExternalInput")
    wukv_d = dt("w_ukv", [128, 1024], F32, "ExternalInput")
    wst_d = dt("w_sT", [128, 1024], F32, "ExternalInput")
    wout_d = dt("w_out", [D_MODEL, D_MODEL], F32, "ExternalInput")
    y_d = dt("y", [NLOC, D_MODEL], F32, "ExternalOutput")

    with ExitStack() as es:
        K = Ctx(nc, es)
        pe, act, dve, pool, sp = K.pe, K.act, K.dve, K.pool, K.sp

        def sbt(stack, name, shape, dty):
            return stack.enter_context(nc.sbuf_tensor("t_" + name, shape, dty))

        ps = es.enter_context(nc.psum_tensor("ps", [128, 4096], F32))
        bank = [Buf(f"bank{i}", excl=True) for i in range(8)]

        def psf(b, c0=0, c1=512):
            return ps[:, b * 512 + c0:b * 512 + c1]

        def psb(b):
            return ps[:, b * 512:(b + 1) * 512].bitcast(BF16)

        QT = sbt(es, "QT", [128, 2, 8, 1024], BF16)
        bQT = [Buf("QT0"), Buf("QT1")]
        gate = sbt(es, "gate", [128, NTL, 512], BF16)
        bgate = Buf("gate")
        mixTB = sbt(es, "mixTB", [128, 4, NLOC], BF16)
        bmixTB = Buf("mixTB")
        ckvnT = sbt(es, "ckvnT", [128, SEQ], BF16)
        bckvnT = Buf("ckvnT")
        kpe_all = sbt(es, "kpe_all", [128, NT, 32], BF16)
        bkpe = Buf("kpe_all")
        sskpe = sbt(es, "sskpe", [128, NT], F32)
        bsskpe = Buf("sskpe")
        identb = sbt(es, "identb", [128, 128], BF16)
        pack1 = sbt(es, "pack1", [128, 171], F32)
        identf = pack1[:, 43:171]
        bident = Buf("ident")
        mhalf = sbt(es, "mhalf", [128, 16], F32)
        bmhalf = Buf("mhalf")
        wukv = sbt(es, "wukv", [128, 1024], BF16)
        bwukv = Buf("wukv")
        gout = pack1[:, 19:27]
        bgout = Buf("gout")

        s_const = K.newsem("s_const")
        s_stage = [K.newsem(f"s_stage{i}") for i in range(4)]
        s_x = [K.newsem(f"s_x{i}") for i in range(4)]
        s_out = [K.newsem(f"s_out{i}") for i in range(3)]

        def pow_rstd(out_ap, in_ap, n, bout, bin_, tmp_ap, btmp, width):
            K.op(pool, lambda: nc.gpsimd.tensor_scalar(out=tmp_ap, in0=in_ap, scalar1=1.0 / n, scalar2=EPS,
                                                       op0=ALU.mult, op1=ALU.add), rd=[bin_], wr=[btmp])
            K.op(pool, lambda: nc.gpsimd.tensor_tensor(out=out_ap, in0=tmp_ap, in1=mhalf[:, 0:width], op=ALU.pow),
                 rd=[btmp, bmhalf], wr=[bout])

        def scale_cast(eng, dst_ap, src_ap, gain_ap, rd, wr):
            if eng is dve:
                f = lambda: nc.vector.tensor_scalar(out=dst_ap, in0=src_ap, scalar1=gain_ap, scalar2=None, op0=ALU.mult)
            elif eng is pool:
                f = lambda: nc.gpsimd.tensor_scalar(out=dst_ap, in0=src_ap, scalar1=gain_ap, scalar2=1.0,
                                                    op0=ALU.mult, op1=ALU.mult)
            else:
                f = lambda: nc.scalar.activation(out=dst_ap, in_=src_ap, func=AF.Copy, scale=gain_ap)
            K.op(eng, f, rd=rd, wr=wr)

        with ExitStack() as pb:
            w_in = sbt(pb, "w_in_b", [128, 8, D_IN], BF16)
            bw_in = Buf("w_in")
            wuq = sbt(pb, "wuq", [128, 2, 768], BF16)
            bwuq = Buf("wuq")
            wst = sbt(pb, "wst", [128, 8, 128], BF16)
            bwst = Buf("wst")
            pack2 = sbt(pb, "pack2", [128, 704], F32)
            gin = pack1[:, 0:8]
            gql = pack1[:, 8:10]
            gkvl = pack1[:, 10:11]
            bsT = pack1[:, 11:19]
            invf = pack1[:, 27:43]
            gq_t = pack2[:, 0:96]
            gk_t = pack2[:, 96:192]
            gvg = pack2[:, 192:704]
            gqk = sbt(pb, "gqk", [128, 96], F32)
            bgqk = Buf("gqk")
            pos_i = sbt(pb, "pos_i", [128, NT], I32)
            bconst = Buf("const")
            CS = sbt(pb, "CS", [128, NT, 32], F32)
            SN = sbt(pb, "SN", [128, NT, 32], F32)
            bCS = Buf("CS")
            bSN = Buf("SN")
            xt = [sbt(pb, f"xt{i}", [128, D_MODEL], F32) for i in range(4)]
            bxt = [Buf(f"xt{i}") for i in range(4)]


            ORDER = list(range(NTL, NT - 2)) + list(range(NTL)) + [NT - 2, NT - 1]
            POS = {t: i for i, t in enumerate(ORDER)}

            def load_x(t):
                s = POS[t] % 4
                K.dma(sp, xt[s][:], x_d[t * 128:(t + 1) * 128, :], s_x[s], wr=[bxt[s]])

            def two(name, shape, dty):
                return [sbt(pb, f"{name}{i}", shape, dty) for i in range(2)], [Buf(f"{name}{i}") for i in range(2)]

            xb, bxb = two("xb", [128, D_MODEL], BF16)
            xT, bxT = two("xT", [128, 8, 128], BF16)
            junk = sbt(pb, "junk", [128, D_MODEL], BF16)
            bjunk = Buf("junk")
            junks = sbt(pb, "junks", [128, 3, 256], BF16)
            bjunks = [Buf("junks0"), Buf("junks1"), Buf("junks2")]
            st = sbt(pb, "st", [128, 2, 64], F32)
            bst = {}

            def stb(name, p):
                if (name, p) not in bst:
                    bst[(name, p)] = Buf(f"{name}{p}")
                return bst[(name, p)]

            C_SSX, C_RSX, C_T0, C_SSQ, C_SSKV, C_T1, C_R2, C_COMB, C_SSOB, C_T2, C_ROB = 0, 1, 2, 3, 4, 6, 8, 10, 12, 13, 14
            C_SSV, C_TV, C_RV = 16, 24, 32
            C_SSQH, C_TQH, C_RQH = 40, 48, 56

            def stc(p, c, w=1):
                return st[:, p, c:c + w]

            ckvn, bckvn = two("ckvn", [128, 128], BF16)
            kr, bkr = two("kr", [128, 32], F32)
            krA = sbt(pb, "krA", [128, 32], F32)
            bkrA = Buf("krA")
            krB = sbt(pb, "krB", [128, 32], F32)
            bkrB = [Buf("krB0"), Buf("krB1")]
            bw_kv = Buf("w_kv")

            consts = [(pack1[:], pack1_d[:, :]), (pos_i[:], pos_d[:, :]), (pack2[:], pack2_d[:, :])]
            def load_consts():
                for o, i in consts:
                    K.dma(pool, o, i, s_const, wr=[Buf("c")])
                bconst.w = (s_const, s_const.n)
                bident.w = bconst.w
                bgout.w = bconst.w

            p0m = ExitStack()
            with ExitStack() as p0:
                stage = [sbt(p0m, f"stage{i}", [128, D_IN], F32) for i in range(3)]
                bstage = [Buf(f"stage{i}") for i in range(3)]
                load_consts()
                for c in range(8):
                    K.dma(sp, stage[0][:, c * 160:(c + 1) * 160], win_d[c * 128:(c + 1) * 128, 256:416], s_stage[0],
                          wr=[Buf("kvslice")])
                bstage[0].w = (s_stage[0], s_stage[0].n)
                for t in ORDER[0:3]:
                    load_x(t)
                witems = []
                for c in range(8):
                    witems.append((win_d[c * 128:(c + 1) * 128, :], D_IN, gin[:, c:c + 1],
                                   [(w_in[:, c, 0:256], 0, 256), (w_in[:, c, 416:D_IN], 416, D_IN)], bw_in))
                for c in range(2):
                    witems.append((wuq_d[c * 128:(c + 1) * 128, :], 768, gql[:, c:c + 1], [(wuq[:, c, :], 0, 768)], bwuq))
                witems.append((wukv_d[:, :], 1024, gkvl[:, 0:1], [(wukv[:], 0, 1024)], bwukv))
                witems.append((wst_d[:, :], 1024, None, [(wst[:].rearrange("p h i -> p (h i)"), 0, 1024)], bwst))

                def w_dma(i):
                    if i < len(witems):
                        src, ncols = witems[i][0], witems[i][1]
                        si = (i + 1) % 3
                        K.dma(sp, stage[si][:, 0:ncols], src, s_stage[si], wr=[bstage[si]])

                def w_conv(i, eng):
                    if i >= len(witems):
                        return
                    _, ncols, gain_ap, dsts, bdst = witems[i]
                    si = (i + 1) % 3
                    for dst_ap, c0, c1 in dsts:
                        if gain_ap is None:
                            K.op(dve, lambda: nc.vector.tensor_copy(out=dst_ap, in_=stage[si][:, c0:c1]),
                                 rd=[bstage[si]], wr=[bdst])
                        else:
                            scale_cast(eng, dst_ap, stage[si][:, c0:c1], gain_ap, [bstage[si], bconst], [bdst])

                w_dma(0)
                w_dma(1)
                K.op(dve, lambda: nc.vector.tensor_copy(out=identb[:], in_=identf[:]), rd=[bconst], wr=[bident])
                K.op(dve, lambda: nc.vector.memset(mhalf[:], -0.5), wr=[bmhalf])
                st_names = ["ssx", "rsx", "t0", "ssq", "sskv", "t1", "t1b", "r2", "comb", "ssob", "t2", "rob",
                            "ssv", "tv", "rv", "ssqh", "tqh", "rqh"]
                K.op(dve, lambda: nc.vector.memset(st[:].rearrange("p a b -> p (a b)"), 1.0),
                     wr=[stb(n_, p_) for n_ in st_names for p_ in range(2)])
                K.op(dve, lambda: nc.vector.tensor_tensor(out=gqk[:], in0=gq_t[:], in1=gk_t[:], op=ALU.mult),
                     rd=[bconst], wr=[bgqk])

                pos_f = sbt(p0m, "pos_f", [128, NT], F32)
                ang = sbt(p0m, "ang", [128, NT, 16], F32)
                tq = sbt(p0m, "tq", [128, NT, 16], F32)
                ki = sbt(p0m, "ki", [128, NT, 16], I32)
                r1 = sbt(p0m, "r1", [128, NT, 16], F32)
                rc = sbt(p0m, "rc", [128, NT, 16], F32)
                kf = tq
                r2 = r1
                mk = tq
                bt = [Buf(f"rp{i}") for i in range(10)]
                bt[4] = bt[2]
                bt[8] = bt[2]
                bt[6] = bt[5]
                K.op(dve, lambda: nc.vector.tensor_copy(out=pos_f[:], in_=pos_i[:]), rd=[bconst], wr=[bt[0]])
                K.op(dve, lambda: nc.vector.tensor_tensor(
                    out=ang[:], in0=pos_f[:].unsqueeze(2).to_broadcast([128, NT, 16]),
                    in1=invf[:].unsqueeze(1).to_broadcast([128, NT, 16]), op=ALU.mult), rd=[bt[0], bconst], wr=[bt[1]])
                K.op(dve, lambda: nc.vector.tensor_scalar(out=tq[:], in0=ang[:], scalar1=1.0 / TWO_PI, scalar2=None,
                                                          op0=ALU.mult), rd=[bt[1]], wr=[bt[2]])
                K.op(dve, lambda: nc.vector.tensor_copy(out=ki[:], in_=tq[:]), rd=[bt[2]], wr=[bt[3]])
                K.op(dve, lambda: nc.vector.tensor_copy(out=kf[:], in_=ki[:]), rd=[bt[3]], wr=[bt[4]])
                K.op(dve, lambda: nc.vector.scalar_tensor_tensor(out=r1[:], in0=kf[:], scalar=-CW1, in1=ang[:],
                                                                 op0=ALU.mult, op1=ALU.add), rd=[bt[4], bt[1]], wr=[bt[5]])
                K.op(dve, lambda: nc.vector.scalar_tensor_tensor(out=r2[:], in0=kf[:], scalar=-CW2, in1=r1[:],
                                                                 op0=ALU.mult, op1=ALU.add), rd=[bt[4], bt[5]], wr=[bt[6]])
                K.op(dve, lambda: nc.vector.tensor_scalar(out=rc[:], in0=r2[:], scalar1=math.pi / 2, scalar2=None,
                                                          op0=ALU.add), rd=[bt[6]], wr=[bt[7]])
                K.op(dve, lambda: nc.vector.tensor_scalar(out=mk[:], in0=rc[:], scalar1=math.pi, scalar2=-TWO_PI,
                                                          op0=ALU.is_gt, op1=ALU.mult), rd=[bt[7]], wr=[bt[8]])
                K.op(dve, lambda: nc.vector.tensor_tensor(out=rc[:], in0=rc[:], in1=mk[:], op=ALU.add),
                     rd=[bt[7], bt[8]], wr=[bt[7]])
                K.op(dve, lambda: nc.vector.tensor_scalar(out=rc[:], in0=rc[:], scalar1=-PI_LO, scalar2=PI_LO,
                                                          op0=ALU.max, op1=ALU.min), rd=[bt[7]], wr=[bt[7]])
                K.op(dve, lambda: nc.vector.tensor_scalar(out=r2[:], in0=r2[:], scalar1=-PI_LO, scalar2=PI_LO,
                                                          op0=ALU.max, op1=ALU.min), rd=[bt[6]], wr=[bt[6]])
                K.op(act, lambda: nc.scalar.activation(out=CS[:, :, 0:16], in_=rc[:], func=AF.Sin), rd=[bt[7]], wr=[bCS])
                K.op(act, lambda: nc.scalar.activation(out=CS[:, :, 16:32], in_=rc[:], func=AF.Sin), rd=[bt[7]], wr=[bCS])
                K.op(act, lambda: nc.scalar.activation(out=SN[:, :, 0:16], in_=r2[:], func=AF.Sin, scale=-1.0),
                     rd=[bt[6]], wr=[bSN])
                K.op(act, lambda: nc.scalar.activation(out=SN[:, :, 16:32], in_=r2[:], func=AF.Sin), rd=[bt[6]], wr=[bSN])
                K.op(dve, lambda: nc.vector.tensor_tensor(
                    out=w_in[:, :, 256:416], in0=stage[0][:, 0:1280].rearrange("p (c n) -> p c n", c=8),
                    in1=gin[:, 0:8].unsqueeze(2).to_broadcast([128, 8, 160]), op=ALU.mult),
                    rd=[bstage[0], bconst], wr=[bw_kv])
                w_dma(2)

            b4lo = b4hi = bank[4]
            b7lo = b7hi = bank[7]
            BLK = {"A": (1, 0, 416), "B": (2, 416, 928), "E": (3, 1952, 2464), "C": (1, 928, 1440), "D": (2, 1440, 1952)}

            def inproj_block(t, name, bk=None):
                p = t % 2
                bk0, c0, c1 = BLK[name]
                bk = bk0 if bk is None else bk
                for c in range(8):
                    K.op(pe, lambda c=c: nc.tensor.matmul(psf(bk, 0, c1 - c0), lhsT=xT[p][:, c, :], rhs=w_in[:, c, c0:c1],
                                                          start=(c == 0), stop=(c == 7)),
                         rd=[bxT[p], bw_in, bw_kv], wr=[bank[bk]], sig=(c == 7))

            def valid(t):
                return 0 <= t < NT

            def loc(t):
                return 0 <= t < NTL

            def A0(t):
                p, s4 = t % 2, POS[t] % 4
                K.op(act, lambda: nc.scalar.activation(out=junk[:], in_=xt[s4][:], func=AF.Square,
                                                       accum_out=stc(p, C_SSX)), rd=[bxt[s4]], wr=[bjunk, stb("ssx", p)])
                if loc(t):
                    K.op(act, lambda: nc.scalar.activation(out=xb[p][:], in_=xt[s4][:], func=AF.Copy),
                         rd=[bxt[s4]], wr=[bxb[p]])
                else:
                    K.op(dve, lambda: nc.vector.tensor_copy(out=xb[p][:], in_=xt[s4][:]), rd=[bxt[s4]], wr=[bxb[p]])

            def A1(t):
                p = t % 2
                pow_rstd(stc(p, C_RSX), stc(p, C_SSX), D_MODEL, stb("rsx", p), stb("ssx", p), stc(p, C_T0), stb("t0", p), 1)

            def A2(t):
                p = t % 2
                for c in range(8):
                    K.op(pe, lambda c=c: nc.tensor.transpose(out=psb(0)[:, c * 128:(c + 1) * 128],
                                                             in_=xb[p][:, c * 128:(c + 1) * 128], identity=identb[:]),
                         rd=[bxb[p], bident], wr=[bank[0]], sig=(c == 7))

            def A3(t):
                p = t % 2
                K.op(dve, lambda: nc.vector.tensor_copy(out=xT[p][:].rearrange("p c t -> p (c t)"), in_=psb(0)[:, :]),
                     rd=[bank[0]], wr=[bxT[p]])
                if POS[t] + 3 < NT:
                    load_x(ORDER[POS[t] + 3])

            def nlb(t):
                return 1 if loc(t) else 1 + (t % 2)

            def B1(t):
                p = t % 2
                rsx = stc(p, C_RSX)
                brsx = stb("rsx", p)
                bk = nlb(t)
                if loc(t):
                    inproj_block(t, "A")
                    o_kv = 256
                    K.op(act, lambda: nc.scalar.activation(out=junks[:, 0, 0:256], in_=psf(1, 0, 256), func=AF.Square,
                                                           scale=rsx, accum_out=stc(p, C_SSQ)),
                         rd=[bank[1], brsx], wr=[bjunks[0], stb("ssq", p)])
                else:
                    for c in range(8):
                        K.op(pe, lambda c=c: nc.tensor.matmul(psf(bk, 0, 160), lhsT=xT[p][:, c, :], rhs=w_in[:, c, 256:416],
                                                              start=(c == 0), stop=(c == 7)),
                             rd=[bxT[p], bw_kv], wr=[bank[bk]], sig=(c == 7))
                    o_kv = 0
                K.op(act, lambda: nc.scalar.activation(out=junks[:, 1, 0:128], in_=psf(bk, o_kv, o_kv + 128), func=AF.Square,
                                                       scale=rsx, accum_out=stc(p, C_SSKV)),
                     rd=[bank[bk], brsx], wr=[bjunks[1], stb("sskv", p)])
                K.op(act, lambda: nc.scalar.activation(out=junks[:, 2, 0:32], in_=psf(bk, o_kv + 128, o_kv + 160),
                                                       func=AF.Square, scale=rsx, accum_out=sskpe[:, t:t + 1]),
                     rd=[bank[bk], brsx], wr=[bjunks[2], Buf("sskpe_col")])
                bsskpe.w = (act.sem, act.sem.n)
                if loc(t):
                    K.op(pool, lambda: nc.gpsimd.tensor_scalar(out=stc(p, C_T1), in0=stc(p, C_SSQ), scalar1=1.0 / 256,
                                                               scalar2=EPS, op0=ALU.mult, op1=ALU.add),
                         rd=[stb("ssq", p)], wr=[stb("t1", p)])
                K.op(pool, lambda: nc.gpsimd.tensor_scalar(out=stc(p, C_T1 + 1), in0=stc(p, C_SSKV), scalar1=1.0 / 128,
                                                           scalar2=EPS, op0=ALU.mult, op1=ALU.add),
                     rd=[stb("sskv", p)], wr=[stb("t1b", p)])
                K.op(pool, lambda: nc.gpsimd.tensor_tensor(out=stc(p, C_R2, 2), in0=stc(p, C_T1, 2), in1=mhalf[:, 0:2],
                                                           op=ALU.pow), rd=[stb("t1", p), stb("t1b", p), bmhalf], wr=[stb("r2", p)])

            def B2(t, bks):
                if not loc(t):
                    return
                p = t % 2
                rsx, brsx = stc(p, C_RSX), stb("rsx", p)
                inproj_block(t, "B", bks[0])
                K.op(act, lambda: nc.scalar.activation(out=gate[:, t, :], in_=psf(bks[0]), func=AF.Silu, scale=rsx),
                     rd=[bank[bks[0]], brsx], wr=[bgate])
                inproj_block(t, "E", bks[1])
                K.op(act, lambda: nc.scalar.activation(out=zb[p][:], in_=psf(bks[1]), func=AF.Silu, scale=rsx),
                     rd=[bank[bks[1]], brsx], wr=[bzb[p]])

            def B3(t):
                p = t % 2
                rsx, brsx = stc(p, C_RSX), stb("rsx", p)
                o_kv = 256 if loc(t) else 0
                bk = nlb(t)
                K.op(dve, lambda: nc.vector.tensor_scalar(out=stc(p, C_COMB, 2), in0=stc(p, C_R2, 2), scalar1=rsx,
                                                          scalar2=None, op0=ALU.mult),
                     rd=[stb("r2", p), brsx], wr=[stb("comb", p)])
                if loc(t):
                    K.op(dve, lambda: nc.vector.tensor_scalar(out=cqn[p][:], in0=psf(1, 0, 256), scalar1=stc(p, C_COMB),
                                                              scalar2=None, op0=ALU.mult),
                         rd=[bank[1], stb("comb", p)], wr=[bcqn[p]])
                K.op(dve, lambda: nc.vector.tensor_scalar(out=ckvn[p][:], in0=psf(bk, o_kv, o_kv + 128),
                                                          scalar1=stc(p, C_COMB + 1), scalar2=None, op0=ALU.mult),
                     rd=[bank[bk], stb("comb", p)], wr=[bckvn[p]])
                K.op(dve, lambda: nc.vector.tensor_scalar(out=kr[p][:], in0=psf(bk, o_kv + 128, o_kv + 160), scalar1=rsx,
                                                          scalar2=None, op0=ALU.mult), rd=[bank[bk], brsx], wr=[bkr[p]])

            def B4(t, bks):
                if not loc(t):
                    return
                p = t % 2
                rsx, brsx = stc(p, C_RSX), stb("rsx", p)
                inproj_block(t, "C", bks[0])
                K.op(act, lambda: nc.scalar.activation(out=ug[p][:], in_=psf(bks[0]), func=AF.Gelu_apprx_tanh, scale=rsx),
                     rd=[bank[bks[0]], brsx], wr=[bug[p]])
                inproj_block(t, "D", bks[1])
                K.op(act, lambda: nc.scalar.activation(out=vg[p][:], in_=psf(bks[1]), func=AF.Gelu_apprx_tanh, scale=rsx),
                     rd=[bank[bks[1]], brsx], wr=[bvg[p]])

            def B5(t):
                p = t % 2
                n = 0
                if loc(t):
                    for c in range(2):
                        K.op(pe, lambda c=c: nc.tensor.transpose(out=psb(4)[:, c * 128:(c + 1) * 128],
                                                                 in_=cqn[p][:, c * 128:(c + 1) * 128], identity=identb[:]),
                             rd=[bcqn[p], bident], wr=[b4lo], sig=False)
                    n = 2
                K.op(pe, lambda: nc.tensor.transpose(out=psb(4)[:, n * 128:(n + 1) * 128], in_=ckvn[p][:],
                                                     identity=identb[:]), rd=[bckvn[p], bident], wr=[b4lo])

            def B6(t):
                p = t % 2
                n = 2 if loc(t) else 0
                if loc(t):
                    K.op(dve, lambda: nc.vector.tensor_copy(out=cqnT[p][:].rearrange("p c t -> p (c t)"),
                                                            in_=psb(4)[:, 0:256]), rd=[b4lo], wr=[bcqnT[p]])
                K.op(dve, lambda: nc.vector.tensor_copy(out=ckvnT[:, t * 128:(t + 1) * 128],
                                                        in_=psb(4)[:, n * 128:(n + 1) * 128]), rd=[b4lo], wr=[bckvnT])
                K.op(dve, lambda: nc.vector.tensor_tensor(out=krA[:], in0=kr[p][:], in1=CS[:, t, :], op=ALU.mult),
                     rd=[bkr[p], bCS], wr=[bkrA])
                K.op(dve, lambda: nc.vector.tensor_tensor(out=krB[:, 0:16], in0=kr[p][:, 16:32], in1=SN[:, t, 0:16],
                                                          op=ALU.mult), rd=[bkr[p], bSN], wr=[bkrB[0]])
                K.op(dve, lambda: nc.vector.tensor_tensor(out=krB[:, 16:32], in0=kr[p][:, 0:16], in1=SN[:, t, 16:32],
                                                          op=ALU.mult), rd=[bkr[p], bSN], wr=[bkrB[1]])
                K.op(dve, lambda: nc.vector.tensor_tensor(out=kpe_all[:, t, :], in0=krA[:], in1=krB[:], op=ALU.add),
                     rd=[bkrA, bkrB[0], bkrB[1]], wr=[bkpe])

            def B7a(t):
                if not loc(t):
                    return
                p = t % 2
                K.op(act, lambda: nc.scalar.activation(out=sq[:], in_=vg[p][:], func=AF.Square),
                     rd=[bvg[p]], wr=[bsq])
                K.op(dve, lambda: nc.vector.tensor_reduce(out=stc(p, C_SSV, 8), in_=sq[:].rearrange("p (h d) -> p h d", h=8),
                                                          axis=AX.X, op=ALU.add), rd=[bsq], wr=[stb("ssv", p)])
                pow_rstd(stc(p, C_RV, 8), stc(p, C_SSV, 8), 64, stb("rv", p), stb("ssv", p), stc(p, C_TV, 8), stb("tv", p), 8)

            def B7b(t):
                p = t % 2
                K.op(dve, lambda: nc.vector.tensor_tensor(out=vt[:], in0=vg[p][:], in1=gvg[:], op=ALU.mult),
                     rd=[bvg[p], bconst], wr=[bvt])
                K.op(dve, lambda: nc.vector.tensor_tensor(
                    out=vn[p][:].rearrange("p (h d) -> p h d", h=8), in0=vt[:].rearrange("p (h d) -> p h d", h=8),
                    in1=stc(p, C_RV, 8).unsqueeze(2).to_broadcast([128, 8, 64]), op=ALU.mult),
                    rd=[bvt, stb("rv", p)], wr=[bvn[p]])

            def C1g(t):
                p = t % 2
                for h in range(8):
                    K.op(pe, lambda h=h: nc.tensor.matmul(psf(5, h * 64, (h + 1) * 64), lhsT=wst[:, h, :],
                                                          rhs=vn[p][:, h * 64:(h + 1) * 64], start=True, stop=True),
                         rd=[bwst, bvn[p]], wr=[bank[5]], sig=(h == 7))

            def C1q(t):
                p = t % 2
                for (ap_out, bb, n0, n1) in ((psf(6), bank[6], 0, 512), (psf(7, 0, 256), b7lo, 512, 768)):
                    for c in range(2):
                        K.op(pe, lambda c=c, ap_out=ap_out, n0=n0, n1=n1: nc.tensor.matmul(
                            ap_out, lhsT=cqnT[p][:, c, :], rhs=wuq[:, c, n0:n1], start=(c == 0), stop=(c == 1)),
                            rd=[bcqnT[p], bwuq], wr=[bb], sig=(c == 1))

            def C2(t):
                p = t % 2
                ob, bob = ob2[p], bob2[p]
                K.op(dve, lambda: nc.vector.tensor_tensor(
                    out=ob[:].rearrange("p (h d) -> p h d", h=8), in0=psf(5).rearrange("p (h d) -> p h d", h=8),
                    in1=bsT[:].unsqueeze(2).to_broadcast([128, 8, 64]), op=ALU.add), rd=[bank[5], bconst], wr=[bob])
                K.op(dve, lambda: nc.vector.tensor_tensor(out=ob[:], in0=ob[:], in1=ug[p][:], op=ALU.mult),
                     rd=[bob, bug[p]], wr=[bob])

            def C2s(t):
                p = t % 2
                ob, bob = ob2[p], bob2[p]
                K.op(act, lambda: nc.scalar.activation(out=junk[:, 0:512], in_=ob[:], func=AF.Square,
                                                       accum_out=stc(p, C_SSOB)),
                     rd=[bob], wr=[bjunk, stb("ssob", p)])

            def C3(t):
                p = t % 2
                K.op(act, lambda: nc.scalar.activation(out=qf[p][:, 0:512], in_=psf(6), func=AF.Copy),
                     rd=[bank[6]], wr=[bqf[p][0]])
                K.op(act, lambda: nc.scalar.activation(out=qf[p][:, 512:768], in_=psf(7, 0, 256), func=AF.Copy),
                     rd=[b7lo], wr=[bqf[p][1]])
                K.op(act, lambda: nc.scalar.activation(out=sqq[p][:, 0:512], in_=psf(6), func=AF.Square),
                     rd=[bank[6]], wr=[bsqq[p][0]])
                K.op(act, lambda: nc.scalar.activation(out=sqq[p][:, 512:768], in_=psf(7, 0, 256), func=AF.Square),
                     rd=[b7lo], wr=[bsqq[p][1]])

            def C4(t):
                p = t % 2
                pow_rstd(stc(p, C_ROB), stc(p, C_SSOB), 512, stb("rob", p), stb("ssob", p), stc(p, C_T2), stb("t2", p), 1)

            def C5(t):
                p = t % 2
                K.op(dve, lambda: nc.vector.tensor_reduce(out=stc(p, C_SSQH, 8),
                                                          in_=sqq[p][:].rearrange("p (h d) -> p h d", h=8),
                                                          axis=AX.X, op=ALU.add), rd=bsqq[p], wr=[stb("ssqh", p)])
                pow_rstd(stc(p, C_RQH, 8), stc(p, C_SSQH, 8), 96, stb("rqh", p), stb("ssqh", p), stc(p, C_TQH, 8),
                         stb("tqh", p), 8)

            def C6(t):
                p = t % 2
                ob, bob = ob2[p], bob2[p]
                K.op(dve, lambda: nc.vector.scalar_tensor_tensor(out=mixB[p][:], in0=ob[:], scalar=stc(p, C_ROB),
                                                                 in1=zb[p][:], op0=ALU.mult, op1=ALU.mult),
                     rd=[bob, stb("rob", p), bzb[p]], wr=[bmixB[p]])

            def C7(t):
                p = t % 2
                qf3 = qf[p][:].rearrange("p (h d) -> p h d", h=8)
                K.op(dve, lambda: nc.vector.tensor_tensor(out=qA[:], in0=qf3[:, :, 64:96],
                                                          in1=CS[:, t, :].unsqueeze(1).to_broadcast([128, 8, 32]),
                                                          op=ALU.mult), rd=bqf[p] + [bCS], wr=[bqA])
                K.op(dve, lambda: nc.vector.tensor_tensor(out=qB[:, :, 0:16], in0=qf3[:, :, 80:96],
                                                          in1=SN[:, t, 0:16].unsqueeze(1).to_broadcast([128, 8, 16]),
                                                          op=ALU.mult), rd=bqf[p] + [bSN], wr=[bqB[0]])
                K.op(dve, lambda: nc.vector.tensor_tensor(out=qB[:, :, 16:32], in0=qf3[:, :, 64:80],
                                                          in1=SN[:, t, 16:32].unsqueeze(1).to_broadcast([128, 8, 16]),
                                                          op=ALU.mult), rd=bqf[p] + [bSN], wr=[bqB[1]])
                K.op(dve, lambda: nc.vector.tensor_tensor(out=qf3[:, :, 64:96], in0=qA[:], in1=qB[:], op=ALU.add),
                     rd=[bqA, bqB[0], bqB[1]], wr=bqf[p])
                K.op(dve, lambda: nc.vector.tensor_tensor(
                    out=qg[:].rearrange("p (h d) -> p h d", h=8), in0=qf3,
                    in1=gqk[:].unsqueeze(1).to_broadcast([128, 8, 96]), op=ALU.mult), rd=bqf[p] + [bgqk], wr=[bqg])
                K.op(dve, lambda: nc.vector.tensor_tensor(
                    out=qn[p][:].rearrange("p (h d) -> p h d", h=8), in0=qg[:].rearrange("p (h d) -> p h d", h=8),
                    in1=stc(p, C_RQH, 8).unsqueeze(2).to_broadcast([128, 8, 96]), op=ALU.mult),
                    rd=[bqg, stb("rqh", p)], wr=[bqn[p]])

            def q_tr(t):
                p = t % 2
                for h in range(8):
                    K.op(pe, lambda h=h: nc.tensor.transpose(out=psb(0)[0:96, h * 128:(h + 1) * 128],
                                                             in_=qn[p][:, h * 96:(h + 1) * 96], identity=identb[:]),
                         rd=[bqn[p], bident], wr=[bank[0]], sig=(h == 7))

            def q_cp(t):
                G, tt = t // 8, t % 8
                K.op(act, lambda: nc.scalar.activation(
                    out=QT[0:96, G, :, tt * 128:(tt + 1) * 128],
                    in_=psb(0)[0:96, :].rearrange("p (h t) -> p h t", h=8), func=AF.Copy), rd=[bank[0]], wr=[bQT[G]])

            def C8(t):
                p = t % 2
                for c in range(4):
                    K.op(pe, lambda c=c: nc.tensor.transpose(out=psb(4)[:, 512 + c * 128:512 + (c + 1) * 128],
                                                             in_=mixB[p][:, c * 128:(c + 1) * 128], identity=identb[:]),
                         rd=[bmixB[p], bident], wr=[b4hi], sig=(c == 3))

            def C9(t):
                K.op(act, lambda: nc.scalar.activation(out=mixTB[:, :, t * 128:(t + 1) * 128],
                                                       in_=psb(4)[:, 512:1024].rearrange("p (c t) -> p c t", c=4),
                                                       func=AF.Copy), rd=[b4hi], wr=[bmixTB])

            def C10(t):
                q_tr(t)
                q_cp(t)

            NLEAD = NT - NTL - 2
            conv_eng = [act, pool]

            def tile_at(pos):
                return ORDER[pos] if 0 <= pos < NT else None

            def isloc(t):
                return t is not None and t < NTL

            def isnl(t):
                return t is not None and t >= NTL

            def step(k):
                ta, tb, tc, td = tile_at(k), tile_at(k - 1), tile_at(k - 2), tile_at(k - 3)
                if isnl(tc):
                    B3(tc)
                if isloc(tc):
                    B7b(tc)
                if isloc(td):
                    C6(td)
                    C7(td)
                if tb is not None:
                    B1(tb)
                if isnl(tc):
                    B5(tc)
                    B6(tc)
                if ta is not None:
                    A2(ta)
                    A1(ta)
                if isloc(tc):
                    C1q(tc)
                    C1g(tc)
                if ta is not None:
                    A3(ta)
                if isloc(tb):
                    B3(tb)
                if isloc(tc):
                    C3(tc)
                    C2(tc)
                if isloc(tb):
                    B4(tb, (2, 3))
                    B7a(tb)
                if isloc(tc):
                    C2s(tc)
                    C4(tc)
                if isloc(td):
                    C8(td)
                    C9(td)
                if isloc(tc):
                    C5(tc)
                if isloc(td):
                    C10(td)
                if tile_at(k + 1) is not None:
                    A0(tile_at(k + 1))
                if isloc(tb):
                    B5(tb)
                    B2(tb, (2, 3))
                    B6(tb)
                w_conv(k, conv_eng[k % 2])
                w_dma(k + 3)

            assert len(witems) <= NLEAD
            A0(ORDER[0])
            for k in range(NLEAD):
                step(k)

            prior = {}
            for b_ in bstage + bt:
                toks = list(b_.r.items()) + ([b_.w] if b_.w is not None else [])
                for s_, v_ in toks:
                    if prior.get(s_, 0) < v_:
                        prior[s_] = v_
            p0m.close()

            cqn, bcqn = two("cqn", [128, 256], BF16)
            cqnT, bcqnT = two("cqnT", [128, 2, 128], BF16)
            ug, bug = two("ug", [128, 512], F32)
            vg, bvg = two("vg", [128, 512], F32)
            zb, bzb = two("zb", [128, 512], F32)
            sq = sbt(pb, "sq", [128, 512], F32)
            bsq = Buf("sq")
            sqq, _ = two("sqq", [128, 768], F32)
            bsqq = [[Buf(f"sqq{i}a"), Buf(f"sqq{i}b")] for i in range(2)]
            vt = sbt(pb, "vt", [128, 512], F32)
            bvt = Buf("vt")
            vn, bvn = two("vn", [128, 512], BF16)
            ob2, bob2 = two("ob", [128, 512], F32)
            mixB, bmixB = two("mixB", [128, 512], BF16)
            qf, _ = two("qf", [128, 768], F32)
            bqf = [[Buf(f"qf{i}a"), Buf(f"qf{i}b")] for i in range(2)]
            qA = sbt(pb, "qA", [128, 8, 32], F32)
            bqA = Buf("qA")
            qB = sbt(pb, "qB", [128, 8, 32], F32)
            bqB = [Buf("qB0"), Buf("qB1")]
            qg = sbt(pb, "qg", [128, 768], F32)
            bqg = Buf("qg")
            qn, bqn = two("qn", [128, 768], BF16)


            for b_ in (bcqn + bcqnT + bug + bvg + bzb + [bsq] + bsqq[0] + bsqq[1] + [bvt] + bvn + bob2 + bmixB
                       + bqf[0] + bqf[1] + [bqA] + bqB + [bqg] + bqn):
                b_.r = dict(prior)

            for k in range(NLEAD, NT + 3):
                step(k)
            K.barrier()

        with ExitStack() as p2:
            PT = [sbt(p2, f"PT{i}", [128, 1024], BF16) for i in range(3)]
            bPT = [Buf(f"PT{i}") for i in range(3)]
            oT = sbt(p2, "oT", [128, 1024], F32)
            boT = [Buf("oT0"), Buf("oT1")]
            oa = sbt(p2, "oa", [128, 8, 512], F32)
            boa = Buf("oa")
            rcp = sbt(p2, "rcp", [128, 8], F32)
            brcp = Buf("rcp")
            st2 = sbt(p2, "st2", [128, 4, 4], F32)
            bst2 = [[Buf(f"st2_{i}_{j}") for j in range(3)] for i in range(4)]
            junk2 = sbt(p2, "junk2", [128, 512], BF16)
            bjunk2 = Buf("junk2")
            mixA0 = sbt(p2, "mixA0", [128, 512], BF16)
            mixA = [mixA0, mixA0]
            bmixA0 = Buf("mixA0")
            bmixA = [bmixA0, bmixA0]
            with ExitStack() as pkv:
                KT = sbt(pkv, "KT", [128, 8, SEQ], BF16)
                bKT = [Buf(f"KT{h}") for h in range(8)]
                Vx = sbt(pkv, "Vx", [128, NT, 8, 65], BF16)
                bVx = Buf("Vx")
                with ExitStack() as pa:
                    kcat = [sbt(pa, f"kcat{i}", [128, 8, 96], BF16) for i in range(2)]
                    bkcat = [[Buf(f"kcat{i}_{j}") for j in range(3)] for i in range(2)]
                    sqk = [oa[:, i, :] for i in range(2)]
                    bsqk2 = [[Buf(f"sqk{i}a"), Buf(f"sqk{i}b")] for i in range(2)]
                    sk = sbt(pa, "sk", [128, 2, 32], F32)
                    bsk = [[Buf(f"sk{i}_{j}") for j in range(4)] for i in range(2)]
                    K.op(dve, lambda: nc.vector.memset(Vx[:].rearrange("p t h d -> p (t h) d")[:, :, 64:65], 1.0), wr=[bVx])
                    KVB = [(0, 1), (2, 3), (4, 5)]

                    def X_pe(t):
                        p = t % 2
                        for j, bk in enumerate(KVB[t % 3]):
                            K.op(pe, lambda j=j, bk=bk: nc.tensor.matmul(psf(bk), lhsT=ckvnT[:, t * 128:(t + 1) * 128],
                                                                         rhs=wukv[:, j * 512:(j + 1) * 512],
                                                                         start=True, stop=True),
                                 rd=[bckvnT, bwukv], wr=[bank[bk]])

                    def X_act(t):
                        p = t % 2
                        bsqk = bsqk2[p]
                        bk0, bk1 = KVB[t % 3]
                        src = ps[:, bk0 * 512:bk0 * 512 + 1024].rearrange("p (h d) -> p h d", h=8)
                        K.op(act, lambda: nc.scalar.activation(out=sqk[p][:].rearrange("p (h d) -> p h d", h=8),
                                                               in_=src[:, :, 0:64], func=AF.Square),
                             rd=[bank[bk0], bank[bk1]], wr=bsqk)
                        K.op(act, lambda: nc.scalar.activation(out=Vx[:, t, :, 0:64], in_=src[:, :, 64:128], func=AF.Copy),
                             rd=[bank[bk0], bank[bk1]], wr=[Buf("vxpart")])
                        bVx.w = (act.sem, act.sem.n)

                    def X_dve(t):
                        p = t % 2
                        bsqk = bsqk2[p]
                        K.op(dve, lambda: nc.vector.tensor_reduce(out=sk[:, p, 0:8],
                                                                  in_=sqk[p][:].rearrange("p (h d) -> p h d", h=8),
                                                                  axis=AX.X, op=ALU.add), rd=bsqk, wr=[bsk[p][0]])
                        K.op(dve, lambda: nc.vector.tensor_scalar(out=sk[:, p, 8:16], in0=sk[:, p, 0:8],
                                                                  scalar1=sskpe[:, t:t + 1], scalar2=None, op0=ALU.add),
                             rd=[bsk[p][0], bsskpe], wr=[bsk[p][1]])
                        pow_rstd(sk[:, p, 24:32], sk[:, p, 8:16], 96, bsk[p][3], bsk[p][1], sk[:, p, 16:24], bsk[p][2], 8)

                    def Y_dve(t):
                        p = t % 2
                        rk = sk[:, p, 24:32]
                        bk0, bk1 = KVB[t % 3]
                        src = ps[:, bk0 * 512:bk0 * 512 + 1024].rearrange("p (h d) -> p h d", h=8)
                        K.op(dve, lambda: nc.vector.tensor_tensor(
                            out=kcat[p][:, :, 0:64], in0=src[:, :, 0:64],
                            in1=rk.unsqueeze(2).to_broadcast([128, 8, 64]), op=ALU.mult),
                            rd=[bank[bk0], bank[bk1], bsk[p][3]], wr=[bkcat[p][0], bkcat[p][1]])
                        K.op(dve, lambda: nc.vector.tensor_tensor(
                            out=kcat[p][:, :, 64:96], in0=kpe_all[:, t, :].unsqueeze(1).to_broadcast([128, 8, 32]),
                            in1=rk.unsqueeze(2).to_broadcast([128, 8, 32]), op=ALU.mult),
                            rd=[bkpe, bsk[p][3]], wr=[bkcat[p][2]])

                    def Y_pe(t):
                        p = t % 2
                        bkT = 6 + p
                        for h in range(8):
                            K.op(pe, lambda h=h: nc.tensor.transpose(out=psb(bkT)[0:96, h * 128:(h + 1) * 128],
                                                                     in_=kcat[p][:, h, :], identity=identb[:]),
                                 rd=bkcat[p] + [bident], wr=[bank[bkT]], sig=(h == 7))

                    def Y_cp(t):
                        p = t % 2
                        bkT = 6 + p
                        K.op(act, lambda: nc.scalar.activation(out=KT[0:96, :, t * 128:(t + 1) * 128],
                                                               in_=psb(bkT)[0:96, :].rearrange("p (h t) -> p h t", h=8),
                                                               func=AF.Copy), rd=[bank[bkT]], wr=bKT)

                    for k in range(NT + 4):
                        if k < NT:
                            X_pe(k)
                        if 0 <= k - 4 < NT:
                            Y_cp(k - 4)
                        if k < NT:
                            X_act(k)
                        if 0 <= k - 1 < NT:
                            X_dve(k - 1)
                        if 0 <= k - 2 < NT:
                            Y_dve(k - 2)
                        if 0 <= k - 3 < NT:
                            Y_pe(k - 3)
                    K.barrier()

                SB = [(0, 1), (2, 3)]
                ACCP = [(4, 5), (6, 7)]
                TRB = (6, 7)

                def accof(G, h):
                    return ACCP[(G * 8 + h) % 2]
                EXP_SCALE = 1.0 / math.sqrt(96.0)
                NIT = 2 * 8 * NT
                bg = []

                def idx(i):
                    return i // (8 * NT), (i // NT) % 8, i % NT

                bufof = {}
                inflight = {}
                issued = [-1]

                def issue_scores(i):
                    while issued[0] + 1 < NIT and issued[0] + 1 <= i + 3:
                        j = issued[0] + 1
                        Gj, hj, _ = idx(j)
                        cands = [SB[0], SB[1]]
                        if i >= 0:
                            Gi, hi, kti = idx(i)
                            if (Gi, hi) == (Gj, hj) and not bg and kti >= 4:
                                cands.append(ACCP[1 - ((Gj * 8 + hj) % 2)])
                        free = [p for p in cands if p not in inflight]
                        if not free:
                            break
                        bufof[j] = free[0]
                        inflight[free[0]] = j
                        scores(j)
                        issued[0] = j

                def scores(i):
                    G, h, kt = idx(i)
                    b0 = bufof[i]
                    for j in range(2):
                        K.op(pe, lambda j=j: nc.tensor.matmul(
                            psf(b0[j]), lhsT=KT[0:96, h, kt * 128:(kt + 1) * 128],
                            rhs=QT[0:96, G, h, j * 512:(j + 1) * 512], start=True, stop=True),
                            rd=[bKT[h], bQT[G]], wr=[bank[b0[j]]], sig=(j == 1))

                def expo(i):
                    b0 = bufof[i]
                    K.op(act, lambda: nc.scalar.activation(out=PT[i % 3][:], in_=ps[:, b0[0] * 512:b0[0] * 512 + 1024],
                                                           func=AF.Exp, scale=EXP_SCALE),
                         rd=[bank[b0[0]], bank[b0[1]]], wr=[bPT[i % 3]], waw_implied=(i >= 3))
                    del inflight[b0]

                def pv(i):
                    G, h, kt = idx(i)
                    ACC = accof(G, h)
                    for j in range(2):
                        K.op(pe, lambda j=j: nc.tensor.matmul(
                            ps[0:65, ACC[j] * 512:(ACC[j] + 1) * 512], lhsT=Vx[:, kt, h, :],
                            rhs=PT[i % 3][:, j * 512:(j + 1) * 512], start=(kt == 0), stop=(kt == NT - 1)),
                            rd=[bVx, bPT[i % 3]], wr=[bank[ACC[j]]], sig=(j == 1))

                def fin_copy(G, h):
                    ACC = accof(G, h)
                    for j in range(2):
                        K.op(dve, lambda j=j: nc.vector.tensor_copy(out=oT[0:65, j * 512:(j + 1) * 512],
                                                                    in_=ps[0:65, ACC[j] * 512:(ACC[j] + 1) * 512]),
                             rd=[bank[ACC[j]]], wr=[boT[j]])

                def fin_tr(G, h):
                    for half in range(2):
                        tb = accof(G, h)[half]
                        for i in range(4):
                            tt = half * 4 + i
                            K.op(pe, lambda i=i, tt=tt, tb=tb: nc.tensor.transpose(
                                out=psf(tb, i * 65, (i + 1) * 65), in_=oT[0:65, tt * 128:(tt + 1) * 128],
                                identity=identf[0:65, 0:65]), rd=boT + [bident], wr=[bank[tb]], sig=(i == 3))
                        v3 = psf(tb, 0, 260).rearrange("p (i d) -> p i d", i=4)
                        K.op(dve, lambda v3=v3, half=half: nc.vector.reciprocal(
                            out=rcp[:, half * 4:(half + 1) * 4].unsqueeze(2), in_=v3[:, :, 64:65]),
                            rd=[bank[tb]], wr=[brcp])
                        K.op(dve, lambda v3=v3, half=half: nc.vector.tensor_tensor(
                            out=oa[:, half * 4:(half + 1) * 4, h * 64:(h + 1) * 64], in0=v3[:, :, 0:64],
                            in1=rcp[:, half * 4:(half + 1) * 4].unsqueeze(2).to_broadcast([128, 4, 64]), op=ALU.mult),
                            rd=[bank[tb], brcp], wr=[boa])

                def epi_a(G, tt):
                    t = G * 8 + tt
                    p = tt % 4
                    K.op(dve, lambda: nc.vector.scalar_tensor_tensor(out=junk2[:], in0=oa[:, tt, :], scalar=1.0,
                                                                     in1=oa[:, tt, :], op0=ALU.mult, op1=ALU.mult,
                                                                     accum_out=st2[:, p, 0:1]),
                         rd=[boa], wr=[bjunk2, bst2[p][0]])
                    pow_rstd(st2[:, p, 2:3], st2[:, p, 0:1], 512, bst2[p][2], bst2[p][0], st2[:, p, 1:2], bst2[p][1], 1)

                def epi_b(G, tt):
                    t = G * 8 + tt
                    p = tt % 4
                    K.op(dve, lambda: nc.vector.scalar_tensor_tensor(out=mixA[0][:], in0=oa[:, tt, :], scalar=st2[:, p, 2:3],
                                                                     in1=gate[:, t, :], op0=ALU.mult, op1=ALU.mult),
                         rd=[boa, bst2[p][2], bgate], wr=[bmixA[0]])

                def epi_c(G, tt, tb):
                    p = tt % 2
                    mixTA = QT[:, G, 0:4, :]
                    for c in range(4):
                        K.op(pe, lambda c=c: nc.tensor.transpose(out=psb(tb)[:, c * 128:(c + 1) * 128],
                                                                 in_=mixA[p][:, c * 128:(c + 1) * 128], identity=identb[:]),
                             rd=[bmixA[p], bident], wr=[bank[tb]], sig=(c == 3))
                    K.op(dve, lambda: nc.vector.tensor_copy(out=mixTA[:, :, tt * 128:(tt + 1) * 128],
                                                            in_=psb(tb)[:, 0:512].rearrange("p (c t) -> p c t", c=4)),
                         rd=[bank[tb]], wr=[bQT[G]])

                woutv = KT[:, 4:6, :].rearrange("p h (c n) -> p (h c) n", n=D_MODEL)
                stgv = [KT[:, 6, i * 2048:(i + 1) * 2048].bitcast(F32) for i in range(2)]
                bwout = Buf("wout")
                bstg = [Buf("stg0"), Buf("stg1")]

                def merged(bufs):
                    r = {}
                    for b_ in bufs:
                        for s_, v_ in b_.r.items():
                            if r.get(s_, 0) < v_:
                                r[s_] = v_
                    return r

                def prep_wout():
                    dead = merged(bKT[4:7])
                    for b_ in [bwout] + bstg:
                        b_.w = bKT[4].w
                        b_.r = dict(dead)
                    for c in range(8):
                        si = c % 2
                        K.dma(sp, stgv[si], wout_d[c * 128:(c + 1) * 128, :], s_stage[si], wr=[bstg[si]])
                        scale_cast(dve if c % 2 == 0 else pool, woutv[:, c, :], stgv[si], gout[:, c:c + 1],
                                   [bstg[si], bgout], [bwout])

                issue_scores(-1)
                for i in range(NIT):
                    G, h, kt = idx(i)
                    if i == NIT - NT + 2:
                        prep_wout()
                    expo(i)
                    issue_scores(i)
                    pv(i)
                    if bg:
                        bg.pop(0)()
                    if kt == NT - 1:
                        fin_copy(G, h)
                        bg.append(lambda: None)
                        bg.append(lambda G=G, h=h: fin_tr(G, h))
                        if h == 7 and G == 0:
                            for k in range(8 + 3):
                                if k < 8:
                                    bg.append(lambda G=G, tt=k: epi_a(G, tt))
                                if 0 <= k - 3 < 8:
                                    bg.append(lambda G=G, tt=k - 3: epi_c(G, tt, TRB[0]))
                                if 0 <= k - 2 < 8:
                                    bg.append(lambda G=G, tt=k - 2: epi_b(G, tt))
                while bg:
                    bg.pop(0)()
                kv_readers = merged(bKT + [bVx, bwout] + bstg)

            with ExitStack() as p3:
                wout = woutv
                xr = [sbt(p3, f"xr{i}", [128, D_MODEL], F32) for i in range(4)]
                bxr = [Buf(f"xr{i}") for i in range(4)]
                yo = [sbt(p3, f"yo{i}", [128, D_MODEL], F32) for i in range(3)]
                byo = [Buf(f"yo{i}") for i in range(3)]
                for b_ in bxr + byo:
                    b_.r = dict(kv_readers)

                def load_xr(t):
                    s = t % 4
                    K.dma(sp, xr[s][:], x_d[t * 128:(t + 1) * 128, :], s_x[s], wr=[bxr[s]])

                for t in range(3):
                    load_xr(t)
                YB = [(0, 1), (2, 3)]

                def outproj(t):
                    G, tt = t // 8, t % 8
                    s4, s3, s2 = t % 4, t % 3, t % 2
                    yb = YB[s2]
                    for nb in range(2):
                        for c in range(8):
                            if c < 4:
                                lhsT = QT[:, G, c, tt * 128:(tt + 1) * 128]
                                rdb = bQT[G]
                            else:
                                lhsT = mixTB[:, c - 4, t * 128:(t + 1) * 128]
                                rdb = bmixTB
                            K.op(pe, lambda lhsT=lhsT, c=c, nb=nb: nc.tensor.matmul(
                                psf(yb[nb]), lhsT=lhsT, rhs=wout[:, c, nb * 512:(nb + 1) * 512],
                                start=(c == 0), stop=(c == 7)), rd=[rdb, bwout], wr=[bank[yb[nb]]], sig=(c == 7))
                        K.op(dve, lambda nb=nb: nc.vector.tensor_tensor(out=yo[s3][:, nb * 512:(nb + 1) * 512],
                                                                        in0=psf(yb[nb]), in1=xr[s4][:, nb * 512:(nb + 1) * 512],
                                                                        op=ALU.add),
                             rd=[bank[yb[nb]], bxr[s4]], wr=[byo[s3]])
                    K.dma(act, y_d[t * 128:(t + 1) * 128, :], yo[s3][:], s_out[s3], rd=[byo[s3]])
                    if t + 3 < NTL:
                        load_xr(t + 3)

                epi_a(1, 0)
                for t in range(NTL):
                    if t < 8:
                        if t + 1 < 8:
                            epi_a(1, t + 1)
                        epi_b(1, t)
                    outproj(t)
                    if t < 8:
                        epi_c(1, t, TRB[t % 2])
                for s in s_out:
                    nc.sync.wait_ge(s.h, s.n)
                K.barrier()
    return nc


_NC_CACHE = {}


def kernel(x, positions, norm_in_g, w_in, q_lora_g, w_uq, kv_lora_g, w_ukv, q_head_g, k_head_g,
           v_gate_g, w_s, b_s, out_a_g, out_b_g, w_out):
    f32 = np.float32
    x = np.asarray(x, dtype=f32)
    positions = np.asarray(positions, dtype=np.int32)
    if "nc" not in _NC_CACHE:
        _NC_CACHE["nc"] = build_program()
    nc = _NC_CACHE["nc"]
    inv_freq = (1.0 / (np.float32(10000.0) ** (np.arange(0, 32, 2, dtype=f32) / np.float32(32)))).astype(f32)
    shared = {
        "pack1": np.ascontiguousarray(np.concatenate([
            np.asarray(norm_in_g, f32).reshape(8, 128).T,
            np.asarray(q_lora_g, f32).reshape(2, 128).T,
            np.asarray(kv_lora_g, f32).reshape(128, 1),
            np.asarray(b_s, f32).T,
            np.concatenate([np.asarray(out_a_g, f32), np.asarray(out_b_g, f32)]).reshape(8, 128).T,
            np.broadcast_to(inv_freq[None, :], (128, 16)),
            np.eye(128, dtype=f32)], axis=1).astype(f32)),
        "pack2": np.ascontiguousarray(np.broadcast_to(np.concatenate([
            np.asarray(q_head_g, f32), np.asarray(k_head_g, f32), np.asarray(v_gate_g, f32).reshape(512)])[None, :],
            (128, 704)).astype(f32)),
        "w_in": np.ascontiguousarray(np.asarray(w_in, f32)),
        "w_uq": np.ascontiguousarray(np.asarray(w_uq, f32)),
        "w_ukv": np.ascontiguousarray(np.asarray(w_ukv, f32)),
        "w_sT": np.ascontiguousarray(np.transpose(np.asarray(w_s, f32), (2, 0, 1)).reshape(128, 1024)),
        "w_out": np.ascontiguousarray(np.asarray(w_out, f32)),
    }
    in_maps = []
    for c in range(8):
        b, hf = c // 2, c % 2
        lo = slice(hf * NLOC, (hf + 1) * NLOC)
        ot = slice((1 - hf) * NLOC, (2 - hf) * NLOC)
        xc = np.ascontiguousarray(np.concatenate([x[b, lo], x[b, ot]], axis=0))
        pc = np.concatenate([positions[b, lo], positions[b, ot]], axis=0)
        pc = np.ascontiguousarray(pc.reshape(NT, 128).T)
        m = dict(shared)
        m["x"] = xc
        m["pos"] = pc
        in_maps.append(m)
    res = run_bass_kernel_spmd(nc, in_maps, core_ids=list(range(8)))
    out = np.empty((4, SEQ, D_MODEL), dtype=f32)
    for c in range(8):
        b, hf = c // 2, c % 2
        out[b, hf * NLOC:(hf + 1) * NLOC, :] = res.results[c]["y"]
    return out
```

```python
import math
from contextlib import ExitStack

import numpy as np
import concourse.bass as bass
import concourse.mybir as mybir
from concourse.bass_utils import run_bass_kernel_spmd

F32 = mybir.dt.float32
BF16 = mybir.dt.bfloat16
I32 = mybir.dt.int32
AF = mybir.ActivationFunctionType
ALU = mybir.AluOpType
AX = mybir.AxisListType

D_MODEL = 1024
SEQ = 4096
NLOC = 2048
NT = 32
NTL = 16
D_IN = 2464
EPS = 1e-6
TWO_PI = 2.0 * math.pi
CW1 = 6.28125
CW2 = TWO_PI - CW1
PI_LO = 3.1415925


class Sem:
    def __init__(self, h, name):
        self.h = h
        self.n = 0
        self.name = name


class Buf:
    __slots__ = ("w", "r", "name", "excl")

    def __init__(self, name="", excl=False):
        self.w = None
        self.r = {}
        self.name = name
        self.excl = excl


class Eng:
    def __init__(self, e, sem, name):
        self.e = e
        self.sem = sem
        self.clock = {}
        self.name = name


class Ctx:
    def __init__(self, nc, es):
        self.nc = nc
        self.es = es
        self.sems = []
        self.snap = {}
        self.pe = Eng(nc.tensor, self.newsem("s_pe"), "pe")
        self.act = Eng(nc.scalar, self.newsem("s_act"), "act")
        self.dve = Eng(nc.vector, self.newsem("s_dve"), "dve")
        self.pool = Eng(nc.gpsimd, self.newsem("s_pool"), "pool")
        self.sp = Eng(nc.sync, None, "sp")
        self.engs = [self.pe, self.act, self.dve, self.pool, self.sp]

    def newsem(self, name):
        s = Sem(self.es.enter_context(self.nc.semaphore(name)), name)
        self.sems.append(s)
        return s

    def _waits(self, eng, rd, wr, skip_same, waw_implied=False):
        need = {}

        def add(tok, same_ok):
            if tok is None:
                return
            s, v = tok
            if skip_same and s is eng.sem and not same_ok:
                return
            if need.get(s, 0) < v:
                need[s] = v

        strict = eng is not self.pe
        for b in rd:
            add(b.w, strict)
            if b.excl:
                for s, v in b.r.items():
                    add((s, v), False)
        for b in wr:
            add(b.w, strict and not waw_implied)
            for s, v in b.r.items():
                add((s, v), strict)
        for s, v in sorted(need.items(), key=lambda kv: kv[0] is eng.sem):
            assert v <= s.n, f"wait on unsignalled token {s.name} {v}>{s.n} ({eng.name})"
            if eng.clock.get(s, 0) >= v:
                continue
            eng.e.wait_ge(s.h, v)
            self._learn(eng, s, v)

    def _learn(self, eng, s, v):
        if eng.clock.get(s, 0) < v:
            eng.clock[s] = v
        for s2, v2 in self.snap.get((s, v), {}).items():
            if eng.clock.get(s2, 0) < v2:
                eng.clock[s2] = v2

    def _record(self, tok, rd, wr):
        s, v = tok
        for b in rd:
            if b.r.get(s, 0) < v:
                b.r[s] = v
        for b in wr:
            b.w = tok
            b.r = {}

    def op(self, eng, fn, rd=(), wr=(), sig=True, waw_implied=False):
        self._waits(eng, rd, wr, True, waw_implied)
        ins = fn()
        if sig:
            eng.sem.n += 1
            ins.then_inc(eng.sem.h, 1)
            tok = (eng.sem, eng.sem.n)
            self.snap[tok] = dict(eng.clock)
        else:
            tok = (eng.sem, eng.sem.n + 1)
        self._record(tok, rd, wr)
        return ins

    def dma(self, eng, out, in_, sem, rd=(), wr=()):
        self._waits(eng, rd, wr, False)
        ins = eng.e.dma_start(out=out, in_=in_)
        sem.n += 16
        ins.then_inc(sem.h, 16)
        self.snap[(sem, sem.n)] = dict(eng.clock)
        self._record((sem, sem.n), rd, wr)
        return ins

    def barrier(self):
        for eng in self.engs:
            for s in self.sems:
                if s.n > 0 and eng.clock.get(s, 0) < s.n:
                    eng.e.wait_ge(s.h, s.n)
                    self._learn(eng, s, s.n)


def build_program():
    nc = bass.Bass("TRN2", target_bir_lowering=False)
    dt = lambda name, shape, dty, kind: nc.dram_tensor(name, shape, dty, kind=kind).ap()
    x_d = dt("x", [SEQ, D_MODEL], F32, "ExternalInput")
    pos_d = dt("pos", [128, NT], I32, "ExternalInput")
    pack1_d = dt("pack1", [128, 171], F32, "ExternalInput")
    pack2_d = dt("pack2", [128, 704], F32, "ExternalInput")
    win_d = dt("w_in", [D_MODEL, D_IN], F32, "ExternalInput")
    wuq_d = dt("w_uq", [256, 768], F32, "ExternalInput")
    wukv_d = dt("w_ukv", [128, 1024], F32, "ExternalInput")
    wst_d = dt("w_sT", [128, 1024], F32, "ExternalInput")
    wout_d = dt("w_out", [D_MODEL, D_MODEL], F32, "ExternalInput")
    y_d = dt("y", [NLOC, D_MODEL], F32, "ExternalOutput")

    with ExitStack() as es:
        K = Ctx(nc, es)
        pe, act, dve, pool, sp = K.pe, K.act, K.dve, K.pool, K.sp

        def sbt(stack, name, shape, dty):
            return stack.enter_context(nc.sbuf_tensor("t_" + name, shape, dty))

        ps = es.enter_context(nc.psum_tensor("ps", [128, 4096], F32))
        bank = [Buf(f"bank{i}", excl=True) for i in range(8)]

        def psf(b, c0=0, c1=512):
            return ps[:, b * 512 + c0:b * 512 + c1]

        def psb(b):
            return ps[:, b * 512:(b + 1) * 512].bitcast(BF16)

        QT = sbt(es, "QT", [128, 2, 8, 1024], BF16)
        bQT = [Buf("QT0"), Buf("QT1")]
        gate = sbt(es, "gate", [128, NTL, 512], BF16)
        bgate = Buf("gate")
        mixTB = sbt(es, "mixTB", [128, 4, NLOC], BF16)
        bmixTB = Buf("mixTB")
        ckvnT = sbt(es, "ckvnT", [128, SEQ], BF16)
        bckvnT = Buf("ckvnT")
        kpe_all = sbt(es, "kpe_all", [128, NT, 32], BF16)
        bkpe = Buf("kpe_all")
        sskpe = sbt(es, "sskpe", [128, NT], F32)
        bsskpe = Buf("sskpe")
        identb = sbt(es, "identb", [128, 128], BF16)
        pack1 = sbt(es, "pack1", [128, 171], F32)
        identf = pack1[:, 43:171]
        bident = Buf("ident")
        mhalf = sbt(es, "mhalf", [128, 16], F32)
        bmhalf = Buf("mhalf")
        wukv = sbt(es, "wukv", [128, 1024], BF16)
        bwukv = Buf("wukv")
        gout = pack1[:, 19:27]
        bgout = Buf("gout")

        s_const = K.newsem("s_const")
        s_stage = [K.newsem(f"s_stage{i}") for i in range(4)]
        s_x = [K.newsem(f"s_x{i}") for i in range(4)]
        s_out = [K.newsem(f"s_out{i}") for i in range(3)]

        def pow_rstd(out_ap, in_ap, n, bout, bin_, tmp_ap, btmp, width):
            K.op(pool, lambda: nc.gpsimd.tensor_scalar(out=tmp_ap, in0=in_ap, scalar1=1.0 / n, scalar2=EPS,
                                                       op0=ALU.mult, op1=ALU.add), rd=[bin_], wr=[btmp])
            K.op(pool, lambda: nc.gpsimd.tensor_tensor(out=out_ap, in0=tmp_ap, in1=mhalf[:, 0:width], op=ALU.pow),
                 rd=[btmp, bmhalf], wr=[bout])

        def scale_cast(eng, dst_ap, src_ap, gain_ap, rd, wr):
            if eng is dve:
                f = lambda: nc.vector.tensor_scalar(out=dst_ap, in0=src_ap, scalar1=gain_ap, scalar2=None, op0=ALU.mult)
            elif eng is pool:
                f = lambda: nc.gpsimd.tensor_scalar(out=dst_ap, in0=src_ap, scalar1=gain_ap, scalar2=1.0,
                                                    op0=ALU.mult, op1=ALU.mult)
            else:
                f = lambda: nc.scalar.activation(out=dst_ap, in_=src_ap, func=AF.Copy, scale=gain_ap)
            K.op(eng, f, rd=rd, wr=wr)

        with ExitStack() as pb:
            w_in = sbt(pb, "w_in_b", [128, 8, D_IN], BF16)
            bw_in = Buf("w_in")
            wuq = sbt(pb, "wuq", [128, 2, 768], BF16)
            bwuq = Buf("wuq")
            wst = sbt(pb, "wst", [128, 8, 128], BF16)
            bwst = Buf("wst")
            pack2 = sbt(pb, "pack2", [128, 704], F32)
            gin = pack1[:, 0:8]
            gql = pack1[:, 8:10]
            gkvl = pack1[:, 10:11]
            bsT = pack1[:, 11:19]
            invf = pack1[:, 27:43]
            gq_t = pack2[:, 0:96]
            gk_t = pack2[:, 96:192]
            gvg = pack2[:, 192:704]
            gqk = sbt(pb, "gqk", [128, 96], F32)
            bgqk = Buf("gqk")
            pos_i = sbt(pb, "pos_i", [128, NT], I32)
            bconst = Buf("const")
            CS = sbt(pb, "CS", [128, NT, 32], F32)
            SN = sbt(pb, "SN", [128, NT, 32], F32)
            bCS = Buf("CS")
            bSN = Buf("SN")
            xt = [sbt(pb, f"xt{i}", [128, D_MODEL], F32) for i in range(4)]
            bxt = [Buf(f"xt{i}") for i in range(4)]


            ORDER = list(range(NTL, NT - 2)) + list(range(NTL)) + [NT - 2, NT - 1]
            POS = {t: i for i, t in enumerate(ORDER)}

            def load_x(t):
                s = POS[t] % 4
                K.dma(sp, xt[s][:], x_d[t * 128:(t + 1) * 128, :], s_x[s], wr=[bxt[s]])

            def two(name, shape, dty):
                return [sbt(pb, f"{name}{i}", shape, dty) for i in range(2)], [Buf(f"{name}{i}") for i in range(2)]

            xb, bxb = two("xb", [128, D_MODEL], BF16)
            xT, bxT = two("xT", [128, 8, 128], BF16)
            junk = sbt(pb, "junk", [128, D_MODEL], BF16)
            bjunk = Buf("junk")
            junks = sbt(pb, "junks", [128, 3, 256], BF16)
            bjunks = [Buf("junks0"), Buf("junks1"), Buf("junks2")]
            st = sbt(pb, "st", [128, 2, 64], F32)
            bst = {}

            def stb(name, p):
                if (name, p) not in bst:
                    bst[(name, p)] = Buf(f"{name}{p}")
                return bst[(name, p)]

            C_SSX, C_RSX, C_T0, C_SSQ, C_SSKV, C_T1, C_R2, C_COMB, C_SSOB, C_T2, C_ROB = 0, 1, 2, 3, 4, 6, 8, 10, 12, 13, 14
            C_SSV, C_TV, C_RV = 16, 24, 32
            C_SSQH, C_TQH, C_RQH = 40, 48, 56

            def stc(p, c, w=1):
                return st[:, p, c:c + w]

            ckvn, bckvn = two("ckvn", [128, 128], BF16)
            kr, bkr = two("kr", [128, 32], F32)
            krA = sbt(pb, "krA", [128, 32], F32)
            bkrA = Buf("krA")
            krB = sbt(pb, "krB", [128, 32], F32)
            bkrB = [Buf("krB0"), Buf("krB1")]
            bw_kv = Buf("w_kv")

            consts = [(pack1[:], pack1_d[:, :]), (pos_i[:], pos_d[:, :]), (pack2[:], pack2_d[:, :])]
            def load_consts():
                for o, i in consts:
                    K.dma(pool, o, i, s_const, wr=[Buf("c")])
                bconst.w = (s_const, s_const.n)
                bident.w = bconst.w
                bgout.w = bconst.w

            p0m = ExitStack()
            with ExitStack() as p0:
                stage = [sbt(p0m, f"stage{i}", [128, D_IN], F32) for i in range(3)]
                bstage = [Buf(f"stage{i}") for i in range(3)]
                load_consts()
                for c in range(8):
                    K.dma(sp, stage[0][:, c * 160:(c + 1) * 160], win_d[c * 128:(c + 1) * 128, 256:416], s_stage[0],
                          wr=[Buf("kvslice")])
                bstage[0].w = (s_stage[0], s_stage[0].n)
                for t in ORDER[0:3]:
                    load_x(t)
                witems = []
                for c in range(8):
                    witems.append((win_d[c * 128:(c + 1) * 128, :], D_IN, gin[:, c:c + 1],
                                   [(w_in[:, c, 0:256], 0, 256), (w_in[:, c, 416:D_IN], 416, D_IN)], bw_in))
                for c in range(2):
                    witems.append((wuq_d[c * 128:(c + 1) * 128, :], 768, gql[:, c:c + 1], [(wuq[:, c, :], 0, 768)], bwuq))
                witems.append((wukv_d[:, :], 1024, gkvl[:, 0:1], [(wukv[:], 0, 1024)], bwukv))
                witems.append((wst_d[:, :], 1024, None, [(wst[:].rearrange("p h i -> p (h i)"), 0, 1024)], bwst))

                def w_dma(i):
                    if i < len(witems):
                        src, ncols = witems[i][0], witems[i][1]
                        si = (i + 1) % 3
                        K.dma(sp, stage[si][:, 0:ncols], src, s_stage[si], wr=[bstage[si]])

                def w_conv(i, eng):
                    if i >= len(witems):
                        return
                    _, ncols, gain_ap, dsts, bdst = witems[i]
                    si = (i + 1) % 3
                    for dst_ap, c0, c1 in dsts:
                        if gain_ap is None:
                            K.op(dve, lambda: nc.vector.tensor_copy(out=dst_ap, in_=stage[si][:, c0:c1]),
                                 rd=[bstage[si]], wr=[bdst])
                        else:
                            scale_cast(eng, dst_ap, stage[si][:, c0:c1], gain_ap, [bstage[si], bconst], [bdst])

                w_dma(0)
                w_dma(1)
                K.op(dve, lambda: nc.vector.tensor_copy(out=identb[:], in_=identf[:]), rd=[bconst], wr=[bident])
                K.op(dve, lambda: nc.vector.memset(mhalf[:], -0.5), wr=[bmhalf])
                st_names = ["ssx", "rsx", "t0", "ssq", "sskv", "t1", "t1b", "r2", "comb", "ssob", "t2", "rob",
                            "ssv", "tv", "rv", "ssqh", "tqh", "rqh"]
                K.op(dve, lambda: nc.vector.memset(st[:].rearrange("p a b -> p (a b)"), 1.0),
                     wr=[stb(n_, p_) for n_ in st_names for p_ in range(2)])
                K.op(dve, lambda: nc.vector.tensor_tensor(out=gqk[:], in0=gq_t[:], in1=gk_t[:], op=ALU.mult),
                     rd=[bconst], wr=[bgqk])

                pos_f = sbt(p0m, "pos_f", [128, NT], F32)
                ang = sbt(p0m, "ang", [128, NT, 16], F32)
                tq = sbt(p0m, "tq", [128, NT, 16], F32)
                ki = sbt(p0m, "ki", [128, NT, 16], I32)
                r1 = sbt(p0m, "r1", [128, NT, 16], F32)
                rc = sbt(p0m, "rc", [128, NT, 16], F32)
                kf = tq
                r2 = r1
                mk = tq
                bt = [Buf(f"rp{i}") for i in range(10)]
                bt[4] = bt[2]
                bt[8] = bt[2]
                bt[6] = bt[5]
                K.op(dve, lambda: nc.vector.tensor_copy(out=pos_f[:], in_=pos_i[:]), rd=[bconst], wr=[bt[0]])
                K.op(dve, lambda: nc.vector.tensor_tensor(
                    out=ang[:], in0=pos_f[:].unsqueeze(2).to_broadcast([128, NT, 16]),
                    in1=invf[:].unsqueeze(1).to_broadcast([128, NT, 16]), op=ALU.mult), rd=[bt[0], bconst], wr=[bt[1]])
                K.op(dve, lambda: nc.vector.tensor_scalar(out=tq[:], in0=ang[:], scalar1=1.0 / TWO_PI, scalar2=None,
                                                          op0=ALU.mult), rd=[bt[1]], wr=[bt[2]])
                K.op(dve, lambda: nc.vector.tensor_copy(out=ki[:], in_=tq[:]), rd=[bt[2]], wr=[bt[3]])
                K.op(dve, lambda: nc.vector.tensor_copy(out=kf[:], in_=ki[:]), rd=[bt[3]], wr=[bt[4]])
                K.op(dve, lambda: nc.vector.scalar_tensor_tensor(out=r1[:], in0=kf[:], scalar=-CW1, in1=ang[:],
                                                                 op0=ALU.mult, op1=ALU.add), rd=[bt[4], bt[1]], wr=[bt[5]])
                K.op(dve, lambda: nc.vector.scalar_tensor_tensor(out=r2[:], in0=kf[:], scalar=-CW2, in1=r1[:],
                                                                 op0=ALU.mult, op1=ALU.add), rd=[bt[4], bt[5]], wr=[bt[6]])
                K.op(dve, lambda: nc.vector.tensor_scalar(out=rc[:], in0=r2[:], scalar1=math.pi / 2, scalar2=None,
                                                          op0=ALU.add), rd=[bt[6]], wr=[bt[7]])
                K.op(dve, lambda: nc.vector.tensor_scalar(out=mk[:], in0=rc[:], scalar1=math.pi, scalar2=-TWO_PI,
                                                          op0=ALU.is_gt, op1=ALU.mult), rd=[bt[7]], wr=[bt[8]])
                K.op(dve, lambda: nc.vector.tensor_tensor(out=rc[:], in0=rc[:], in1=mk[:], op=ALU.add),
                     rd=[bt[7], bt[8]], wr=[bt[7]])
                K.op(dve, lambda: nc.vector.tensor_scalar(out=rc[:], in0=rc[:], scalar1=-PI_LO, scalar2=PI_LO,
                                                          op0=ALU.max, op1=ALU.min), rd=[bt[7]], wr=[bt[7]])
                K.op(dve, lambda: nc.vector.tensor_scalar(out=r2[:], in0=r2[:], scalar1=-PI_LO, scalar2=PI_LO,
                                                          op0=ALU.max, op1=ALU.min), rd=[bt[6]], wr=[bt[6]])
                K.op(act, lambda: nc.scalar.activation(out=CS[:, :, 0:16], in_=rc[:], func=AF.Sin), rd=[bt[7]], wr=[bCS])
                K.op(act, lambda: nc.scalar.activation(out=CS[:, :, 16:32], in_=rc[:], func=AF.Sin), rd=[bt[7]], wr=[bCS])
                K.op(act, lambda: nc.scalar.activation(out=SN[:, :, 0:16], in_=r2[:], func=AF.Sin, scale=-1.0),
                     rd=[bt[6]], wr=[bSN])
                K.op(act, lambda: nc.scalar.activation(out=SN[:, :, 16:32], in_=r2[:], func=AF.Sin), rd=[bt[6]], wr=[bSN])
                K.op(dve, lambda: nc.vector.tensor_tensor(
                    out=w_in[:, :, 256:416], in0=stage[0][:, 0:1280].rearrange("p (c n) -> p c n", c=8),
                    in1=gin[:, 0:8].unsqueeze(2).to_broadcast([128, 8, 160]), op=ALU.mult),
                    rd=[bstage[0], bconst], wr=[bw_kv])
                w_dma(2)

            b4lo = b4hi = bank[4]
            b7lo = b7hi = bank[7]
            BLK = {"A": (1, 0, 416), "B": (2, 416, 928), "E": (3, 1952, 2464), "C": (1, 928, 1440), "D": (2, 1440, 1952)}

            def inproj_block(t, name, bk=None):
                p = t % 2
                bk0, c0, c1 = BLK[name]
                bk = bk0 if bk is None else bk
                for c in range(8):
                    K.op(pe, lambda c=c: nc.tensor.matmul(psf(bk, 0, c1 - c0), lhsT=xT[p][:, c, :], rhs=w_in[:, c, c0:c1],
                                                          start=(c == 0), stop=(c == 7)),
                         rd=[bxT[p], bw_in, bw_kv], wr=[bank[bk]], sig=(c == 7))

            def valid(t):
                return 0 <= t < NT

            def loc(t):
                return 0 <= t < NTL

            def A0(t):
                p, s4 = t % 2, POS[t] % 4
                K.op(act, lambda: nc.scalar.activation(out=junk[:], in_=xt[s4][:], func=AF.Square,
                                                       accum_out=stc(p, C_SSX)), rd=[bxt[s4]], wr=[bjunk, stb("ssx", p)])
                if loc(t):
                    K.op(act, lambda: nc.scalar.activation(out=xb[p][:], in_=xt[s4][:], func=AF.Copy),
                         rd=[bxt[s4]], wr=[bxb[p]])
                else:
                    K.op(dve, lambda: nc.vector.tensor_copy(out=xb[p][:], in_=xt[s4][:]), rd=[bxt[s4]], wr=[bxb[p]])

            def A1(t):
                p = t % 2
                pow_rstd(stc(p, C_RSX), stc(p, C_SSX), D_MODEL, stb("rsx", p), stb("ssx", p), stc(p, C_T0), stb("t0", p), 1)

            def A2(t):
                p = t % 2
                for c in range(8):
                    K.op(pe, lambda c=c: nc.tensor.transpose(out=psb(0)[:, c * 128:(c + 1) * 128],
                                                             in_=xb[p][:, c * 128:(c + 1) * 128], identity=identb[:]),
                         rd=[bxb[p], bident], wr=[bank[0]], sig=(c == 7))

            def A3(t):
                p = t % 2
                K.op(dve, lambda: nc.vector.tensor_copy(out=xT[p][:].rearrange("p c t -> p (c t)"), in_=psb(0)[:, :]),
                     rd=[bank[0]], wr=[bxT[p]])
                if POS[t] + 3 < NT:
                    load_x(ORDER[POS[t] + 3])

            def nlb(t):
                return 1 if loc(t) else 1 + (t % 2)

            def B1(t):
                p = t % 2
                rsx = stc(p, C_RSX)
                brsx = stb("rsx", p)
                bk = nlb(t)
                if loc(t):
                    inproj_block(t, "A")
                    o_kv = 256
                    K.op(act, lambda: nc.scalar.activation(out=junks[:, 0, 0:256], in_=psf(1, 0, 256), func=AF.Square,
                                                           scale=rsx, accum_out=stc(p, C_SSQ)),
                         rd=[bank[1], brsx], wr=[bjunks[0], stb("ssq", p)])
                else:
                    for c in range(8):
                        K.op(pe, lambda c=c: nc.tensor.matmul(psf(bk, 0, 160), lhsT=xT[p][:, c, :], rhs=w_in[:, c, 256:416],
                                                              start=(c == 0), stop=(c == 7)),
                             rd=[bxT[p], bw_kv], wr=[bank[bk]], sig=(c == 7))
                    o_kv = 0
                K.op(act, lambda: nc.scalar.activation(out=junks[:, 1, 0:128], in_=psf(bk, o_kv, o_kv + 128), func=AF.Square,
                                                       scale=rsx, accum_out=stc(p, C_SSKV)),
                     rd=[bank[bk], brsx], wr=[bjunks[1], stb("sskv", p)])
                K.op(act, lambda: nc.scalar.activation(out=junks[:, 2, 0:32], in_=psf(bk, o_kv + 128, o_kv + 160),
                                                       func=AF.Square, scale=rsx, accum_out=sskpe[:, t:t + 1]),
                     rd=[bank[bk], brsx], wr=[bjunks[2], Buf("sskpe_col")])
                bsskpe.w = (act.sem, act.sem.n)
                if loc(t):
                    K.op(pool, lambda: nc.gpsimd.tensor_scalar(out=stc(p, C_T1), in0=stc(p, C_SSQ), scalar1=1.0 / 256,
                                                               scalar2=EPS, op0=ALU.mult, op1=ALU.add),
                         rd=[stb("ssq", p)], wr=[stb("t1", p)])
                K.op(pool, lambda: nc.gpsimd.tensor_scalar(out=stc(p, C_T1 + 1), in0=stc(p, C_SSKV), scalar1=1.0 / 128,
                                                           scalar2=EPS, op0=ALU.mult, op1=ALU.add),
                     rd=[stb("sskv", p)], wr=[stb("t1b", p)])
                K.op(pool, lambda: nc.gpsimd.tensor_tensor(out=stc(p, C_R2, 2), in0=stc(p, C_T1, 2), in1=mhalf[:, 0:2],
                                                           op=ALU.pow), rd=[stb("t1", p), stb("t1b", p), bmhalf], wr=[stb("r2", p)])

            def B2(t, bks):
                if not loc(t):
                    return
                p = t % 2
                rsx, brsx = stc(p, C_RSX), stb("rsx", p)
                inproj_block(t, "B", bks[0])
                K.op(act, lambda: nc.scalar.activation(out=gate[:, t, :], in_=psf(bks[0]), func=AF.Silu, scale=rsx),
                     rd=[bank[bks[0]], brsx], wr=[bgate])
                inproj_block(t, "E", bks[1])
                K.op(act, lambda: nc.scalar.activation(out=zb[p][:], in_=psf(bks[1]), func=AF.Silu, scale=rsx),
                     rd=[bank[bks[1]], brsx], wr=[bzb[p]])

            def B3(t):
                p = t % 2
                rsx, brsx = stc(p, C_RSX), stb("rsx", p)
                o_kv = 256 if loc(t) else 0
                bk = nlb(t)
                K.op(dve, lambda: nc.vector.tensor_scalar(out=stc(p, C_COMB, 2), in0=stc(p, C_R2, 2), scalar1=rsx,
                                                          scalar2=None, op0=ALU.mult),
                     rd=[stb("r2", p), brsx], wr=[stb("comb", p)])
                if loc(t):
                    K.op(dve, lambda: nc.vector.tensor_scalar(out=cqn[p][:], in0=psf(1, 0, 256), scalar1=stc(p, C_COMB),
                                                              scalar2=None, op0=ALU.mult),
                         rd=[bank[1], stb("comb", p)], wr=[bcqn[p]])
                K.op(dve, lambda: nc.vector.tensor_scalar(out=ckvn[p][:], in0=psf(bk, o_kv, o_kv + 128),
                                                          scalar1=stc(p, C_COMB + 1), scalar2=None, op0=ALU.mult),
                     rd=[bank[bk], stb("comb", p)], wr=[bckvn[p]])
                K.op(dve, lambda: nc.vector.tensor_scalar(out=kr[p][:], in0=psf(bk, o_kv + 128, o_kv + 160), scalar1=rsx,
                                                          scalar2=None, op0=ALU.mult), rd=[bank[bk], brsx], wr=[bkr[p]])

            def B4(t, bks):
                if not loc(t):
                    return
                p = t % 2
                rsx, brsx = stc(p, C_RSX), stb("rsx", p)
                inproj_block(t, "C", bks[0])
                K.op(act, lambda: nc.scalar.activation(out=ug[p][:], in_=psf(bks[0]), func=AF.Gelu_apprx_tanh, scale=rsx),
                     rd=[bank[bks[0]], brsx], wr=[bug[p]])
                inproj_block(t, "D", bks[1])
                K.op(act, lambda: nc.scalar.activation(out=vg[p][:], in_=psf(bks[1]), func=AF.Gelu_apprx_tanh, scale=rsx),
                     rd=[bank[bks[1]], brsx], wr=[bvg[p]])

            def B5(t):
                p = t % 2
                n = 0
                if loc(t):
                    for c in range(2):
                        K.op(pe, lambda c=c: nc.tensor.transpose(out=psb(4)[:, c * 128:(c + 1) * 128],
                                                                 in_=cqn[p][:, c * 128:(c + 1) * 128], identity=identb[:]),
                             rd=[bcqn[p], bident], wr=[b4lo], sig=False)
                    n = 2
                K.op(pe, lambda: nc.tensor.transpose(out=psb(4)[:, n * 128:(n + 1) * 128], in_=ckvn[p][:],
                                                     identity=identb[:]), rd=[bckvn[p], bident], wr=[b4lo])

            def B6(t):
                p = t % 2
                n = 2 if loc(t) else 0
                if loc(t):
                    K.op(dve, lambda: nc.vector.tensor_copy(out=cqnT[p][:].rearrange("p c t -> p (c t)"),
                                                            in_=psb(4)[:, 0:256]), rd=[b4lo], wr=[bcqnT[p]])
                K.op(dve, lambda: nc.vector.tensor_copy(out=ckvnT[:, t * 128:(t + 1) * 128],
                                                        in_=psb(4)[:, n * 128:(n + 1) * 128]), rd=[b4lo], wr=[bckvnT])
                K.op(dve, lambda: nc.vector.tensor_tensor(out=krA[:], in0=kr[p][:], in1=CS[:, t, :], op=ALU.mult),
                     rd=[bkr[p], bCS], wr=[bkrA])
                K.op(dve, lambda: nc.vector.tensor_tensor(out=krB[:, 0:16], in0=kr[p][:, 16:32], in1=SN[:, t, 0:16],
                                                          op=ALU.mult), rd=[bkr[p], bSN], wr=[bkrB[0]])
                K.op(dve, lambda: nc.vector.tensor_tensor(out=krB[:, 16:32], in0=kr[p][:, 0:16], in1=SN[:, t, 16:32],
                                                          op=ALU.mult), rd=[bkr[p], bSN], wr=[bkrB[1]])
                K.op(dve, lambda: nc.vector.tensor_tensor(out=kpe_all[:, t, :], in0=krA[:], in1=krB[:], op=ALU.add),
                     rd=[bkrA, bkrB[0], bkrB[1]], wr=[bkpe])

            def B7a(t):
                if not loc(t):
                    return
                p = t % 2
                K.op(act, lambda: nc.scalar.activation(out=sq[:], in_=vg[p][:], func=AF.Square),
                     rd=[bvg[p]], wr=[bsq])
                K.op(dve, lambda: nc.vector.tensor_reduce(out=stc(p, C_SSV, 8), in_=sq[:].rearrange("p (h d) -> p h d", h=8),
                                                          axis=AX.X, op=ALU.add), rd=[bsq], wr=[stb("ssv", p)])
                pow_rstd(stc(p, C_RV, 8), stc(p, C_SSV, 8), 64, stb("rv", p), stb("ssv", p), stc(p, C_TV, 8), stb("tv", p), 8)

            def B7b(t):
                p = t % 2
                K.op(dve, lambda: nc.vector.tensor_tensor(out=vt[:], in0=vg[p][:], in1=gvg[:], op=ALU.mult),
                     rd=[bvg[p], bconst], wr=[bvt])
                K.op(dve, lambda: nc.vector.tensor_tensor(
                    out=vn[p][:].rearrange("p (h d) -> p h d", h=8), in0=vt[:].rearrange("p (h d) -> p h d", h=8),
                    in1=stc(p, C_RV, 8).unsqueeze(2).to_broadcast([128, 8, 64]), op=ALU.mult),
                    rd=[bvt, stb("rv", p)], wr=[bvn[p]])

            def C1g(t):
                p = t % 2
                for h in range(8):
                    K.op(pe, lambda h=h: nc.tensor.matmul(psf(5, h * 64, (h + 1) * 64), lhsT=wst[:, h, :],
                                                          rhs=vn[p][:, h * 64:(h + 1) * 64], start=True, stop=True),
                         rd=[bwst, bvn[p]], wr=[bank[5]], sig=(h == 7))

            def C1q(t):
                p = t % 2
                for (ap_out, bb, n0, n1) in ((psf(6), bank[6], 0, 512), (psf(7, 0, 256), b7lo, 512, 768)):
                    for c in range(2):
                        K.op(pe, lambda c=c, ap_out=ap_out, n0=n0, n1=n1: nc.tensor.matmul(
                            ap_out, lhsT=cqnT[p][:, c, :], rhs=wuq[:, c, n0:n1], start=(c == 0), stop=(c == 1)),
                            rd=[bcqnT[p], bwuq], wr=[bb], sig=(c == 1))

            def C2(t):
                p = t % 2
                ob, bob = ob2[p], bob2[p]
                K.op(dve, lambda: nc.vector.tensor_tensor(
                    out=ob[:].rearrange("p (h d) -> p h d", h=8), in0=psf(5).rearrange("p (h d) -> p h d", h=8),
                    in1=bsT[:].unsqueeze(2).to_broadcast([128, 8, 64]), op=ALU.add), rd=[bank[5], bconst], wr=[bob])
                K.op(dve, lambda: nc.vector.tensor_tensor(out=ob[:], in0=ob[:], in1=ug[p][:], op=ALU.mult),
                     rd=[bob, bug[p]], wr=[bob])
                K.op(dve, lambda: nc.vector.scalar_tensor_tensor(out=vt[:], in0=ob[:], scalar=1.0, in1=ob[:],
                                                                 op0=ALU.mult, op1=ALU.mult, accum_out=stc(p, C_SSOB)),
                     rd=[bob], wr=[bvt, stb("ssob", p)])

            def C3(t):
                p = t % 2
                K.op(act, lambda: nc.scalar.activation(out=qf[p][:, 0:512], in_=psf(6), func=AF.Copy),
                     rd=[bank[6]], wr=[bqf[p][0]])
                K.op(act, lambda: nc.scalar.activation(out=qf[p][:, 512:768], in_=psf(7, 0, 256), func=AF.Copy),
                     rd=[b7lo], wr=[bqf[p][1]])
                K.op(act, lambda: nc.scalar.activation(out=sqq[p][:, 0:512], in_=psf(6), func=AF.Square),
                     rd=[bank[6]], wr=[bsqq[p][0]])
                K.op(act, lambda: nc.scalar.activation(out=sqq[p][:, 512:768], in_=psf(7, 0, 256), func=AF.Square),
                     rd=[b7lo], wr=[bsqq[p][1]])

            def C4(t):
                p = t % 2
                pow_rstd(stc(p, C_ROB), stc(p, C_SSOB), 512, stb("rob", p), stb("ssob", p), stc(p, C_T2), stb("t2", p), 1)

            def C5(t):
                p = t % 2
                K.op(dve, lambda: nc.vector.tensor_reduce(out=stc(p, C_SSQH, 8),
                                                          in_=sqq[p][:].rearrange("p (h d) -> p h d", h=8),
                                                          axis=AX.X, op=ALU.add), rd=bsqq[p], wr=[stb("ssqh", p)])
                pow_rstd(stc(p, C_RQH, 8), stc(p, C_SSQH, 8), 96, stb("rqh", p), stb("ssqh", p), stc(p, C_TQH, 8),
                         stb("tqh", p), 8)

            def C6(t):
                p = t % 2
                ob, bob = ob2[p], bob2[p]
                K.op(dve, lambda: nc.vector.scalar_tensor_tensor(out=mixB[p][:], in0=ob[:], scalar=stc(p, C_ROB),
                                                                 in1=zb[p][:], op0=ALU.mult, op1=ALU.mult),
                     rd=[bob, stb("rob", p), bzb[p]], wr=[bmixB[p]])

            def C7(t):
                p = t % 2
                qf3 = qf[p][:].rearrange("p (h d) -> p h d", h=8)
                K.op(dve, lambda: nc.vector.tensor_tensor(out=qA[:], in0=qf3[:, :, 64:96],
                                                          in1=CS[:, t, :].unsqueeze(1).to_broadcast([128, 8, 32]),
                                                          op=ALU.mult), rd=bqf[p] + [bCS], wr=[bqA])
                K.op(dve, lambda: nc.vector.tensor_tensor(out=qB[:, :, 0:16], in0=qf3[:, :, 80:96],
                                                          in1=SN[:, t, 0:16].unsqueeze(1).to_broadcast([128, 8, 16]),
                                                          op=ALU.mult), rd=bqf[p] + [bSN], wr=[bqB[0]])
                K.op(dve, lambda: nc.vector.tensor_tensor(out=qB[:, :, 16:32], in0=qf3[:, :, 64:80],
                                                          in1=SN[:, t, 16:32].unsqueeze(1).to_broadcast([128, 8, 16]),
                                                          op=ALU.mult), rd=bqf[p] + [bSN], wr=[bqB[1]])
                K.op(dve, lambda: nc.vector.tensor_tensor(out=qf3[:, :, 64:96], in0=qA[:], in1=qB[:], op=ALU.add),
                     rd=[bqA, bqB[0], bqB[1]], wr=bqf[p])
                K.op(dve, lambda: nc.vector.tensor_tensor(
                    out=qg[:].rearrange("p (h d) -> p h d", h=8), in0=qf3,
                    in1=gqk[:].unsqueeze(1).to_broadcast([128, 8, 96]), op=ALU.mult), rd=bqf[p] + [bgqk], wr=[bqg])
                K.op(dve, lambda: nc.vector.tensor_tensor(
                    out=qn[p][:].rearrange("p (h d) -> p h d", h=8), in0=qg[:].rearrange("p (h d) -> p h d", h=8),
                    in1=stc(p, C_RQH, 8).unsqueeze(2).to_broadcast([128, 8, 96]), op=ALU.mult),
                    rd=[bqg, stb("rqh", p)], wr=[bqn[p]])

            def q_tr(t):
                p = t % 2
                for h in range(8):
                    K.op(pe, lambda h=h: nc.tensor.transpose(out=psb(0)[0:96, h * 128:(h + 1) * 128],
                                                             in_=qn[p][:, h * 96:(h + 1) * 96], identity=identb[:]),
                         rd=[bqn[p], bident], wr=[bank[0]], sig=(h == 7))

            def q_cp(t):
                G, tt = t // 8, t % 8
                K.op(act, lambda: nc.scalar.activation(
                    out=QT[0:96, G, :, tt * 128:(tt + 1) * 128],
                    in_=psb(0)[0:96, :].rearrange("p (h t) -> p h t", h=8), func=AF.Copy), rd=[bank[0]], wr=[bQT[G]])

            def C8(t):
                p = t % 2
                for c in range(4):
                    K.op(pe, lambda c=c: nc.tensor.transpose(out=psb(4)[:, 512 + c * 128:512 + (c + 1) * 128],
                                                             in_=mixB[p][:, c * 128:(c + 1) * 128], identity=identb[:]),
                         rd=[bmixB[p], bident], wr=[b4hi], sig=(c == 3))

            def C9(t):
                K.op(act, lambda: nc.scalar.activation(out=mixTB[:, :, t * 128:(t + 1) * 128],
                                                       in_=psb(4)[:, 512:1024].rearrange("p (c t) -> p c t", c=4),
                                                       func=AF.Copy), rd=[b4hi], wr=[bmixTB])

            def C10(t):
                q_tr(t)
                q_cp(t)

            NLEAD = NT - NTL - 2
            conv_eng = [act, pool]

            def tile_at(pos):
                return ORDER[pos] if 0 <= pos < NT else None

            def isloc(t):
                return t is not None and t < NTL

            def isnl(t):
                return t is not None and t >= NTL

            def step(k):
                ta, tb, tc, td = tile_at(k), tile_at(k - 1), tile_at(k - 2), tile_at(k - 3)
                if isnl(tc):
                    B3(tc)
                if isloc(tc):
                    B7b(tc)
                if isloc(td):
                    C6(td)
                    C7(td)
                if tb is not None:
                    B1(tb)
                if isnl(tc):
                    B5(tc)
                    B6(tc)
                if ta is not None:
                    A2(ta)
                    A1(ta)
                if isloc(tc):
                    C1q(tc)
                    C1g(tc)
                if ta is not None:
                    A3(ta)
                if isloc(tb):
                    B3(tb)
                if isloc(tc):
                    C3(tc)
                    C2(tc)
                    C4(tc)
                if isloc(tb):
                    B4(tb, (2, 3))
                    B7a(tb)
                if isloc(td):
                    C8(td)
                    C9(td)
                if isloc(tc):
                    C5(tc)
                if isloc(td):
                    C10(td)
                if tile_at(k + 1) is not None:
                    A0(tile_at(k + 1))
                if isloc(tb):
                    B5(tb)
                    B2(tb, (2, 3))
                    B6(tb)
                w_conv(k, conv_eng[k % 2])
                w_dma(k + 3)

            assert len(witems) <= NLEAD
            A0(ORDER[0])
            for k in range(NLEAD):
                step(k)

            prior = {}
            for b_ in bstage + bt:
                toks = list(b_.r.items()) + ([b_.w] if b_.w is not None else [])
                for s_, v_ in toks:
                    if prior.get(s_, 0) < v_:
                        prior[s_] = v_
            p0m.close()

            cqn, bcqn = two("cqn", [128, 256], BF16)
            cqnT, bcqnT = two("cqnT", [128, 2, 128], BF16)
            ug, bug = two("ug", [128, 512], F32)
            vg, bvg = two("vg", [128, 512], F32)
            zb, bzb = two("zb", [128, 512], F32)
            sq = sbt(pb, "sq", [128, 512], F32)
            bsq = Buf("sq")
            sqq, _ = two("sqq", [128, 768], F32)
            bsqq = [[Buf(f"sqq{i}a"), Buf(f"sqq{i}b")] for i in range(2)]
            vt = sbt(pb, "vt", [128, 512], F32)
            bvt = Buf("vt")
            vn, bvn = two("vn", [128, 512], BF16)
            ob2, bob2 = two("ob", [128, 512], F32)
            mixB, bmixB = two("mixB", [128, 512], BF16)
            qf, _ = two("qf", [128, 768], F32)
            bqf = [[Buf(f"qf{i}a"), Buf(f"qf{i}b")] for i in range(2)]
            qA = sbt(pb, "qA", [128, 8, 32], F32)
            bqA = Buf("qA")
            qB = sbt(pb, "qB", [128, 8, 32], F32)
            bqB = [Buf("qB0"), Buf("qB1")]
            qg = sbt(pb, "qg", [128, 768], F32)
            bqg = Buf("qg")
            qn, bqn = two("qn", [128, 768], BF16)


            for b_ in (bcqn + bcqnT + bug + bvg + bzb + [bsq] + bsqq[0] + bsqq[1] + [bvt] + bvn + bob2 + bmixB
                       + bqf[0] + bqf[1] + [bqA] + bqB + [bqg] + bqn):
                b_.r = dict(prior)

            for k in range(NLEAD, NT + 3):
                step(k)
            K.barrier()

        with ExitStack() as p2:
            PT = [sbt(p2, f"PT{i}", [128, 1024], BF16) for i in range(4)]
            bPT = [Buf(f"PT{i}") for i in range(4)]
            oT = sbt(p2, "oT", [128, 1024], F32)
            boT = [Buf("oT0"), Buf("oT1")]
            oa = sbt(p2, "oa", [128, 8, 512], F32)
            boa = Buf("oa")
            rcp = sbt(p2, "rcp", [128, 8], F32)
            brcp = Buf("rcp")
            st2 = sbt(p2, "st2", [128, 4, 4], F32)
            bst2 = [[Buf(f"st2_{i}_{j}") for j in range(3)] for i in range(4)]
            junk2 = sbt(p2, "junk2", [128, 512], BF16)
            bjunk2 = Buf("junk2")
            mixA0 = sbt(p2, "mixA0", [128, 512], BF16)
            mixA = [mixA0, mixA0]
            bmixA0 = Buf("mixA0")
            bmixA = [bmixA0, bmixA0]
            with ExitStack() as pkv:
                KT = sbt(pkv, "KT", [128, 8, SEQ], BF16)
                bKT = [Buf(f"KT{h}") for h in range(8)]
                Vx = sbt(pkv, "Vx", [128, NT, 8, 65], BF16)
                bVx = Buf("Vx")
                with ExitStack() as pa:
                    kcat = [sbt(pa, f"kcat{i}", [128, 8, 96], BF16) for i in range(2)]
                    bkcat = [[Buf(f"kcat{i}_{j}") for j in range(3)] for i in range(2)]
                    sqk = [oa[:, i, :] for i in range(2)]
                    bsqk2 = [[Buf(f"sqk{i}a"), Buf(f"sqk{i}b")] for i in range(2)]
                    sk = sbt(pa, "sk", [128, 2, 32], F32)
                    bsk = [[Buf(f"sk{i}_{j}") for j in range(4)] for i in range(2)]
                    K.op(dve, lambda: nc.vector.memset(Vx[:].rearrange("p t h d -> p (t h) d")[:, :, 64:65], 1.0), wr=[bVx])
                    KVB = [(0, 1), (2, 3), (4, 5)]

                    def X_pe(t):
                        p = t % 2
                        for j, bk in enumerate(KVB[t % 3]):
                            K.op(pe, lambda j=j, bk=bk: nc.tensor.matmul(psf(bk), lhsT=ckvnT[:, t * 128:(t + 1) * 128],
                                                                         rhs=wukv[:, j * 512:(j + 1) * 512],
                                                                         start=True, stop=True),
                                 rd=[bckvnT, bwukv], wr=[bank[bk]])

                    def X_act(t):
                        p = t % 2
                        bsqk = bsqk2[p]
                        bk0, bk1 = KVB[t % 3]
                        src = ps[:, bk0 * 512:bk0 * 512 + 1024].rearrange("p (h d) -> p h d", h=8)
                        K.op(act, lambda: nc.scalar.activation(out=sqk[p][:].rearrange("p (h d) -> p h d", h=8),
                                                               in_=src[:, :, 0:64], func=AF.Square),
                             rd=[bank[bk0], bank[bk1]], wr=bsqk)
                        K.op(act, lambda: nc.scalar.activation(out=Vx[:, t, :, 0:64], in_=src[:, :, 64:128], func=AF.Copy),
                             rd=[bank[bk0], bank[bk1]], wr=[Buf("vxpart")])
                        bVx.w = (act.sem, act.sem.n)

                    def X_dve(t):
                        p = t % 2
                        bsqk = bsqk2[p]
                        K.op(dve, lambda: nc.vector.tensor_reduce(out=sk[:, p, 0:8],
                                                                  in_=sqk[p][:].rearrange("p (h d) -> p h d", h=8),
                                                                  axis=AX.X, op=ALU.add), rd=bsqk, wr=[bsk[p][0]])
                        K.op(dve, lambda: nc.vector.tensor_scalar(out=sk[:, p, 8:16], in0=sk[:, p, 0:8],
                                                                  scalar1=sskpe[:, t:t + 1], scalar2=None, op0=ALU.add),
                             rd=[bsk[p][0], bsskpe], wr=[bsk[p][1]])
                        pow_rstd(sk[:, p, 24:32], sk[:, p, 8:16], 96, bsk[p][3], bsk[p][1], sk[:, p, 16:24], bsk[p][2], 8)

                    def Y_dve(t):
                        p = t % 2
                        rk = sk[:, p, 24:32]
                        bk0, bk1 = KVB[t % 3]
                        src = ps[:, bk0 * 512:bk0 * 512 + 1024].rearrange("p (h d) -> p h d", h=8)
                        K.op(dve, lambda: nc.vector.tensor_tensor(
                            out=kcat[p][:, :, 0:64], in0=src[:, :, 0:64],
                            in1=rk.unsqueeze(2).to_broadcast([128, 8, 64]), op=ALU.mult),
                            rd=[bank[bk0], bank[bk1], bsk[p][3]], wr=[bkcat[p][0], bkcat[p][1]])
                        K.op(dve, lambda: nc.vector.tensor_tensor(
                            out=kcat[p][:, :, 64:96], in0=kpe_all[:, t, :].unsqueeze(1).to_broadcast([128, 8, 32]),
                            in1=rk.unsqueeze(2).to_broadcast([128, 8, 32]), op=ALU.mult),
                            rd=[bkpe, bsk[p][3]], wr=[bkcat[p][2]])

                    def Y_pe(t):
                        p = t % 2
                        bkT = 6 + p
                        for h in range(8):
                            K.op(pe, lambda h=h: nc.tensor.transpose(out=psb(bkT)[0:96, h * 128:(h + 1) * 128],
                                                                     in_=kcat[p][:, h, :], identity=identb[:]),
                                 rd=bkcat[p] + [bident], wr=[bank[bkT]], sig=(h == 7))

                    def Y_cp(t):
                        p = t % 2
                        bkT = 6 + p
                        K.op(act, lambda: nc.scalar.activation(out=KT[0:96, :, t * 128:(t + 1) * 128],
                                                               in_=psb(bkT)[0:96, :].rearrange("p (h t) -> p h t", h=8),
                                                               func=AF.Copy), rd=[bank[bkT]], wr=bKT)

                    for k in range(NT + 4):
                        if k < NT:
                            X_pe(k)
                        if 0 <= k - 4 < NT:
                            Y_cp(k - 4)
                        if k < NT:
                            X_act(k)
                        if 0 <= k - 1 < NT:
                            X_dve(k - 1)
                        if 0 <= k - 2 < NT:
                            Y_dve(k - 2)
                        if 0 <= k - 3 < NT:
                            Y_pe(k - 3)
                    K.barrier()

                SB = [(0, 1), (2, 3)]
                ACCP = [(4, 5), (6, 7)]
                TRB = (6, 7)

                def accof(G, h):
                    return ACCP[(G * 8 + h) % 2]
                EXP_SCALE = 1.0 / math.sqrt(96.0)
                NIT = 2 * 8 * NT
                bg = []

                def idx(i):
                    return i // (8 * NT), (i // NT) % 8, i % NT

                bufof = {}
                inflight = {}
                issued = [-1]

                def issue_scores(i):
                    while issued[0] + 1 < NIT and issued[0] + 1 <= i + 3:
                        j = issued[0] + 1
                        Gj, hj, _ = idx(j)
                        cands = [SB[0], SB[1]]
                        if i >= 0:
                            Gi, hi, kti = idx(i)
                            if (Gi, hi) == (Gj, hj) and not bg and kti >= 4:
                                cands.append(ACCP[1 - ((Gj * 8 + hj) % 2)])
                        free = [p for p in cands if p not in inflight]
                        if not free:
                            break
                        bufof[j] = free[0]
                        inflight[free[0]] = j
                        scores(j)
                        issued[0] = j

                def scores(i):
                    G, h, kt = idx(i)
                    b0 = bufof[i]
                    for j in range(2):
                        K.op(pe, lambda j=j: nc.tensor.matmul(
                            psf(b0[j]), lhsT=KT[0:96, h, kt * 128:(kt + 1) * 128],
                            rhs=QT[0:96, G, h, j * 512:(j + 1) * 512], start=True, stop=True),
                            rd=[bKT[h], bQT[G]], wr=[bank[b0[j]]], sig=(j == 1))

                def expo(i):
                    b0 = bufof[i]
                    K.op(act, lambda: nc.scalar.activation(out=PT[i % 4][:], in_=ps[:, b0[0] * 512:b0[0] * 512 + 1024],
                                                           func=AF.Exp, scale=EXP_SCALE),
                         rd=[bank[b0[0]], bank[b0[1]]], wr=[bPT[i % 4]], waw_implied=(i >= 4))
                    del inflight[b0]

                def pv(i):
                    G, h, kt = idx(i)
                    ACC = accof(G, h)
                    for j in range(2):
                        K.op(pe, lambda j=j: nc.tensor.matmul(
                            ps[0:65, ACC[j] * 512:(ACC[j] + 1) * 512], lhsT=Vx[:, kt, h, :],
                            rhs=PT[i % 4][:, j * 512:(j + 1) * 512], start=(kt == 0), stop=(kt == NT - 1)),
                            rd=[bVx, bPT[i % 4]], wr=[bank[ACC[j]]], sig=(j == 1))

                def fin_copy(G, h):
                    ACC = accof(G, h)
                    for j in range(2):
                        K.op(dve, lambda j=j: nc.vector.tensor_copy(out=oT[0:65, j * 512:(j + 1) * 512],
                                                                    in_=ps[0:65, ACC[j] * 512:(ACC[j] + 1) * 512]),
                             rd=[bank[ACC[j]]], wr=[boT[j]])

                def fin_tr(G, h):
                    for half in range(2):
                        tb = accof(G, h)[half]
                        for i in range(4):
                            tt = half * 4 + i
                            K.op(pe, lambda i=i, tt=tt, tb=tb: nc.tensor.transpose(
                                out=psf(tb, i * 65, (i + 1) * 65), in_=oT[0:65, tt * 128:(tt + 1) * 128],
                                identity=identf[0:65, 0:65]), rd=boT + [bident], wr=[bank[tb]], sig=(i == 3))
                        v3 = psf(tb, 0, 260).rearrange("p (i d) -> p i d", i=4)
                        K.op(dve, lambda v3=v3, half=half: nc.vector.reciprocal(
                            out=rcp[:, half * 4:(half + 1) * 4].unsqueeze(2), in_=v3[:, :, 64:65]),
                            rd=[bank[tb]], wr=[brcp])
                        K.op(dve, lambda v3=v3, half=half: nc.vector.tensor_tensor(
                            out=oa[:, half * 4:(half + 1) * 4, h * 64:(h + 1) * 64], in0=v3[:, :, 0:64],
                            in1=rcp[:, half * 4:(half + 1) * 4].unsqueeze(2).to_broadcast([128, 4, 64]), op=ALU.mult),
                            rd=[bank[tb], brcp], wr=[boa])

                def epi_a(G, tt):
                    t = G * 8 + tt
                    p = tt % 4
                    K.op(dve, lambda: nc.vector.scalar_tensor_tensor(out=junk2[:], in0=oa[:, tt, :], scalar=1.0,
                                                                     in1=oa[:, tt, :], op0=ALU.mult, op1=ALU.mult,
                                                                     accum_out=st2[:, p, 0:1]),
                         rd=[boa], wr=[bjunk2, bst2[p][0]])
                    pow_rstd(st2[:, p, 2:3], st2[:, p, 0:1], 512, bst2[p][2], bst2[p][0], st2[:, p, 1:2], bst2[p][1], 1)

                def epi_b(G, tt):
                    t = G * 8 + tt
                    p = tt % 4
                    K.op(dve, lambda: nc.vector.scalar_tensor_tensor(out=mixA[0][:], in0=oa[:, tt, :], scalar=st2[:, p, 2:3],
                                                                     in1=gate[:, t, :], op0=ALU.mult, op1=ALU.mult),
                         rd=[boa, bst2[p][2], bgate], wr=[bmixA[0]])

                def epi_c(G, tt, tb):
                    p = tt % 2
                    mixTA = QT[:, G, 0:4, :]
                    for c in range(4):
                        K.op(pe, lambda c=c: nc.tensor.transpose(out=psb(tb)[:, c * 128:(c + 1) * 128],
                                                                 in_=mixA[p][:, c * 128:(c + 1) * 128], identity=identb[:]),
                             rd=[bmixA[p], bident], wr=[bank[tb]], sig=(c == 3))
                    K.op(dve, lambda: nc.vector.tensor_copy(out=mixTA[:, :, tt * 128:(tt + 1) * 128],
                                                            in_=psb(tb)[:, 0:512].rearrange("p (c t) -> p c t", c=4)),
                         rd=[bank[tb]], wr=[bQT[G]])

                woutv = KT[:, 4:6, :].rearrange("p h (c n) -> p (h c) n", n=D_MODEL)
                stgv = [KT[:, 6, i * 2048:(i + 1) * 2048].bitcast(F32) for i in range(2)]
                bwout = Buf("wout")
                bstg = [Buf("stg0"), Buf("stg1")]

                def merged(bufs):
                    r = {}
                    for b_ in bufs:
                        for s_, v_ in b_.r.items():
                            if r.get(s_, 0) < v_:
                                r[s_] = v_
                    return r

                def prep_wout():
                    dead = merged(bKT[4:7])
                    for b_ in [bwout] + bstg:
                        b_.w = bKT[4].w
                        b_.r = dict(dead)
                    for c in range(8):
                        si = c % 2
                        K.dma(sp, stgv[si], wout_d[c * 128:(c + 1) * 128, :], s_stage[si], wr=[bstg[si]])
                        scale_cast(dve if c % 2 == 0 else pool, woutv[:, c, :], stgv[si], gout[:, c:c + 1],
                                   [bstg[si], bgout], [bwout])

                issue_scores(-1)
                for i in range(NIT):
                    G, h, kt = idx(i)
                    if i == NIT - NT + 2:
                        prep_wout()
                    expo(i)
                    issue_scores(i)
                    pv(i)
                    if bg:
                        bg.pop(0)()
                    if kt == NT - 1:
                        fin_copy(G, h)
                        bg.append(lambda: None)
                        bg.append(lambda G=G, h=h: fin_tr(G, h))
                        if h == 7 and G == 0:
                            for k in range(8 + 3):
                                if k < 8:
                                    bg.append(lambda G=G, tt=k: epi_a(G, tt))
                                if 0 <= k - 3 < 8:
                                    bg.append(lambda G=G, tt=k - 3: epi_c(G, tt, TRB[0]))
                                if 0 <= k - 2 < 8:
                                    bg.append(lambda G=G, tt=k - 2: epi_b(G, tt))
                while bg:
                    bg.pop(0)()
                kv_readers = merged(bKT + [bVx, bwout] + bstg)

            with ExitStack() as p3:
                wout = woutv
                xr = [sbt(p3, f"xr{i}", [128, D_MODEL], F32) for i in range(4)]
                bxr = [Buf(f"xr{i}") for i in range(4)]
                yo = [sbt(p3, f"yo{i}", [128, D_MODEL], F32) for i in range(3)]
                byo = [Buf(f"yo{i}") for i in range(3)]
                for b_ in bxr + byo:
                    b_.r = dict(kv_readers)

                def load_xr(t):
                    s = t % 4
                    K.dma(sp, xr[s][:], x_d[t * 128:(t + 1) * 128, :], s_x[s], wr=[bxr[s]])

                for t in range(3):
                    load_xr(t)
                YB = [(0, 1), (2, 3)]

                def outproj(t):
                    G, tt = t // 8, t % 8
                    s4, s3, s2 = t % 4, t % 3, t % 2
                    yb = YB[s2]
                    for nb in range(2):
                        for c in range(8):
                            if c < 4:
                                lhsT = QT[:, G, c, tt * 128:(tt + 1) * 128]
                                rdb = bQT[G]
                            else:
                                lhsT = mixTB[:, c - 4, t * 128:(t + 1) * 128]
                                rdb = bmixTB
                            K.op(pe, lambda lhsT=lhsT, c=c, nb=nb: nc.tensor.matmul(
                                psf(yb[nb]), lhsT=lhsT, rhs=wout[:, c, nb * 512:(nb + 1) * 512],
                                start=(c == 0), stop=(c == 7)), rd=[rdb, bwout], wr=[bank[yb[nb]]], sig=(c == 7))
                        K.op(dve, lambda nb=nb: nc.vector.tensor_tensor(out=yo[s3][:, nb * 512:(nb + 1) * 512],
                                                                        in0=psf(yb[nb]), in1=xr[s4][:, nb * 512:(nb + 1) * 512],
                                                                        op=ALU.add),
                             rd=[bank[yb[nb]], bxr[s4]], wr=[byo[s3]])
                    K.dma(act, y_d[t * 128:(t + 1) * 128, :], yo[s3][:], s_out[s3], rd=[byo[s3]])
                    if t + 3 < NTL:
                        load_xr(t + 3)

                epi_a(1, 0)
                for t in range(NTL):
                    if t < 8:
                        if t + 1 < 8:
                            epi_a(1, t + 1)
                        epi_b(1, t)
                    outproj(t)
                    if t < 8:
                        epi_c(1, t, TRB[t % 2])
                for s in s_out:
                    nc.sync.wait_ge(s.h, s.n)
                K.barrier()
    return nc


_NC_CACHE = {}


def kernel(x, positions, norm_in_g, w_in, q_lora_g, w_uq, kv_lora_g, w_ukv, q_head_g, k_head_g,
           v_gate_g, w_s, b_s, out_a_g, out_b_g, w_out):
    f32 = np.float32
    x = np.asarray(x, dtype=f32)
    positions = np.asarray(positions, dtype=np.int32)
    if "nc" not in _NC_CACHE:
        _NC_CACHE["nc"] = build_program()
    nc = _NC_CACHE["nc"]
    inv_freq = (1.0 / (np.float32(10000.0) ** (np.arange(0, 32, 2, dtype=f32) / np.float32(32)))).astype(f32)
    shared = {
        "pack1": np.ascontiguousarray(np.concatenate([
            np.asarray(norm_in_g, f32).reshape(8, 128).T,
            np.asarray(q_lora_g, f32).reshape(2, 128).T,
            np.asarray(kv_lora_g, f32).reshape(128, 1),
            np.asarray(b_s, f32).T,
            np.concatenate([np.asarray(out_a_g, f32), np.asarray(out_b_g, f32)]).reshape(8, 128).T,
            np.broadcast_to(inv_freq[None, :], (128, 16)),
            np.eye(128, dtype=f32)], axis=1).astype(f32)),
        "pack2": np.ascontiguousarray(np.broadcast_to(np.concatenate([
            np.asarray(q_head_g, f32), np.asarray(k_head_g, f32), np.asarray(v_gate_g, f32).reshape(512)])[None, :],
            (128, 704)).astype(f32)),
        "w_in": np.ascontiguousarray(np.asarray(w_in, f32)),
        "w_uq": np.ascontiguousarray(np.asarray(w_uq, f32)),
        "w_ukv": np.ascontiguousarray(np.asarray(w_ukv, f32)),
        "w_sT": np.ascontiguousarray(np.transpose(np.asarray(w_s, f32), (2, 0, 1)).reshape(128, 1024)),
        "w_out": np.ascontiguousarray(np.asarray(w_out, f32)),
    }
    in_maps = []
    for c in range(8):
        b, hf = c // 2, c % 2
        lo = slice(hf * NLOC, (hf + 1) * NLOC)
        ot = slice((1 - hf) * NLOC, (2 - hf) * NLOC)
        xc = np.ascontiguousarray(np.concatenate([x[b, lo], x[b, ot]], axis=0))
        pc = np.concatenate([positions[b, lo], positions[b, ot]], axis=0)
        pc = np.ascontiguousarray(pc.reshape(NT, 128).T)
        m = dict(shared)
        m["x"] = xc
        m["pos"] = pc
        in_maps.append(m)
    res = run_bass_kernel_spmd(nc, in_maps, core_ids=list(range(8)))
    out = np.empty((4, SEQ, D_MODEL), dtype=f32)
    for c in range(8):
        b, hf = c // 2, c % 2
        out[b, hf * NLOC:(hf + 1) * NLOC, :] = res.results[c]["y"]
    return out
```
